# Optimizing a Trainium2 kernel written in Bass

```python
import math
import jax, jax.numpy as jnp
from jax import lax
import numpy as np

D_MODEL = 1024
BATCH = 8
SEQ = 2048
DEPTH = 1

GRID_W = 64
NA_HEADS = 8
NA_HEAD_DIM = 64
NA_WIN_ROWS = 8
NA_WIN_COLS = 16
DIFF_HEADS = 4
DIFF_QK_DIM = 64
DIFF_V_DIM = 2 * DIFF_QK_DIM
Q_BLOCK = 128
MEM_TOKENS = 256
MEM_HEADS = 4
MEM_HEAD_DIM = 128
D_FF = 2816
N_BRANCH = 3
NORM_EPS = 1e-6

NA_WIDTH = NA_HEADS * NA_HEAD_DIM
DIFF_QK_WIDTH = DIFF_HEADS * 2 * DIFF_QK_DIM
DIFF_V_WIDTH = DIFF_HEADS * DIFF_V_DIM
MEM_WIDTH = MEM_HEADS * MEM_HEAD_DIM
IN_WIDTH = 3 * NA_WIDTH + 2 * DIFF_QK_WIDTH + DIFF_V_WIDTH + MEM_WIDTH

kernel_name = "hybrid_gated_na_diffattn_memory_macaron"


def rmsnorm(x, g):
    xf = x.astype(jnp.float32)
    y = xf * lax.rsqrt(jnp.mean(xf * xf, axis=-1, keepdims=True) + NORM_EPS)
    return (y * g.astype(jnp.float32)).astype(x.dtype)


def swiglu(x, w_gate, w_up, w_down):
    return (jax.nn.silu(x @ w_gate) * (x @ w_up)) @ w_down


def neighbourhood_attention(q, k, v, rpb):
    b, s, h, d = q.shape
    rows = s // GRID_W
    wr = min(NA_WIN_ROWS, rows)
    wc = min(NA_WIN_COLS, GRID_W)
    to_grid = lambda t: t.reshape(b, rows, GRID_W, h, d).transpose(0, 3, 1, 2, 4)
    qg, kg, vg = to_grid(q), to_grid(k), to_grid(v)
    cols = jnp.arange(GRID_W)
    col_start = jnp.clip(cols - wc // 2, 0, GRID_W - wc)
    col_idx = col_start[:, None] + jnp.arange(wc)[None, :]
    dc = col_idx - cols[:, None] + (NA_WIN_COLS - 1)
    row_ids = jnp.arange(rows)
    row_start = jnp.clip(row_ids - wr // 2, 0, rows - wr)
    scale = d ** -0.5

    def one_row(args):
        q_row, r, rs = args
        k_band = lax.dynamic_slice_in_dim(kg, rs, wr, axis=2)
        v_band = lax.dynamic_slice_in_dim(vg, rs, wr, axis=2)
        k_win = k_band[:, :, :, col_idx, :]
        v_win = v_band[:, :, :, col_idx, :]
        logits = jnp.einsum('bhqd,bhrqcd->bhqrc', q_row, k_win).astype(jnp.float32) * scale
        dr = rs + jnp.arange(wr) - r + (NA_WIN_ROWS - 1)
        bias = rpb[:, dr[None, :, None], dc[:, None, :]]
        logits = logits + bias.astype(jnp.float32)[None]
        p = jax.nn.softmax(logits.reshape(b, h, GRID_W, wr * wc), axis=-1)
        p = p.reshape(b, h, GRID_W, wr, wc).astype(v.dtype)
        return jnp.einsum('bhqrc,bhrqcd->bhqd', p, v_win)

    out = lax.map(one_row, (qg.transpose(2, 0, 1, 3, 4), row_ids, row_start))
    return out.transpose(1, 0, 3, 2, 4).reshape(b, s, h * d)


def differential_attention(q1, q2, k1, k2, v, lam, slopes):
    b, h, s, dk = q1.shape
    dv = v.shape[-1]
    nb = s // Q_BLOCK
    scale = dk ** -0.5
    kpos = jnp.arange(s).astype(jnp.float32)

    def one_block(args):
        i, q1b, q2b = args
        qpos = (i * Q_BLOCK + jnp.arange(Q_BLOCK)).astype(jnp.float32)
        alibi = -slopes[:, None, None] * jnp.abs(qpos[:, None] - kpos[None, :])[None]
        p1 = jax.nn.softmax(jnp.einsum('bhqd,bhkd->bhqk', q1b, k1).astype(jnp.float32) * scale + alibi, axis=-1)
        p2 = jax.nn.softmax(jnp.einsum('bhqd,bhkd->bhqk', q2b, k2).astype(jnp.float32) * scale + alibi, axis=-1)
        w = (p1 - lam * p2).astype(v.dtype)
        return jnp.einsum('bhqk,bhkd->bhqd', w, v)

    blk = lambda t: t.reshape(b, h, nb, Q_BLOCK, dk).transpose(2, 0, 1, 3, 4)
    out = lax.map(one_block, (jnp.arange(nb), blk(q1), blk(q2)))
    return out.transpose(1, 2, 0, 3, 4).reshape(b, h, s, dv)


def memory_attention(q, k, v):
    b, s, h, d = q.shape
    logits = jnp.einsum('bshd,bmhd->bhsm', q, k).astype(jnp.float32) * (d ** -0.5)
    p = jax.nn.softmax(logits, axis=-1).astype(v.dtype)
    return jnp.einsum('bhsm,bmhd->bshd', p, v).reshape(b, s, h * d)


def setup_inputs(seed: int = 0) -> dict:
    key = jax.random.key(seed)
    ks = jax.random.split(key, 32)
    f32 = jnp.float32
    nrm = lambda k, shape, scale: jax.random.normal(k, shape, f32) * scale
    gain = lambda k, shape: 1.0 + 0.05 * jax.random.normal(k, shape, f32)
    L, D = DEPTH, D_MODEL
    return {
        "x": jax.random.normal(ks[0], (BATCH, SEQ, D), f32),
        "mem": jax.random.normal(ks[1], (BATCH, MEM_TOKENS, D), f32),
        "ffn1_norm": gain(ks[2], (L, D)),
        "ffn1_w_gate": nrm(ks[3], (L, D, D_FF), D ** -0.5),
        "ffn1_w_up": nrm(ks[4], (L, D, D_FF), D ** -0.5),
        "ffn1_w_down": nrm(ks[5], (L, D_FF, D), D_FF ** -0.5),
        "mix_norm": gain(ks[6], (L, D)),
        "w_in": nrm(ks[7], (L, D, IN_WIDTH), D ** -0.5),
        "na_rpb": nrm(ks[8], (L, NA_HEADS, 2 * NA_WIN_ROWS - 1, 2 * NA_WIN_COLS - 1), 0.1),
        "diff_lambda_q1": nrm(ks[9], (L, DIFF_QK_DIM), 0.1),
        "diff_lambda_k1": nrm(ks[10], (L, DIFF_QK_DIM), 0.1),
        "diff_lambda_q2": nrm(ks[11], (L, DIFF_QK_DIM), 0.1),
        "diff_lambda_k2": nrm(ks[12], (L, DIFF_QK_DIM), 0.1),
        "diff_subln": gain(ks[13], (L, DIFF_V_DIM)),
        "mem_norm": gain(ks[14], (L, D)),
        "w_mem_kv": nrm(ks[15], (L, D, 2 * MEM_WIDTH), D ** -0.5),
        "w_gate": nrm(ks[16], (L, D, N_BRANCH * D), D ** -0.5),
        "b_gate": nrm(ks[17], (L, N_BRANCH * D), 0.01),
        "w_br_na": nrm(ks[18], (L, NA_WIDTH, D), NA_WIDTH ** -0.5),
        "w_br_diff": nrm(ks[19], (L, DIFF_V_WIDTH, D), DIFF_V_WIDTH ** -0.5),
        "w_br_mem": nrm(ks[20], (L, MEM_WIDTH, D), MEM_WIDTH ** -0.5),
        "w_out": nrm(ks[21], (L, D, D), D ** -0.5),
        "ffn2_norm": gain(ks[22], (L, D)),
        "ffn2_w_gate": nrm(ks[23], (L, D, D_FF), D ** -0.5),
        "ffn2_w_up": nrm(ks[24], (L, D, D_FF), D ** -0.5),
        "ffn2_w_down": nrm(ks[25], (L, D_FF, D), D_FF ** -0.5),
        "final_norm": gain(ks[26], (D,)),
    }


def reference(x, mem, ffn1_norm, ffn1_w_gate, ffn1_w_up, ffn1_w_down, mix_norm, w_in, na_rpb,
              diff_lambda_q1, diff_lambda_k1, diff_lambda_q2, diff_lambda_k2, diff_subln,
              mem_norm, w_mem_kv, w_gate, b_gate, w_br_na, w_br_diff, w_br_mem, w_out,
              ffn2_norm, ffn2_w_gate, ffn2_w_up, ffn2_w_down, final_norm):
    b, s, d_model = x.shape
    m = mem.shape[1]
    slopes = jnp.asarray([2.0 ** (-8.0 * (i + 1) / DIFF_HEADS) for i in range(DIFF_HEADS)], jnp.float32)
    o_nq = 0
    o_nk = o_nq + NA_WIDTH
    o_nv = o_nk + NA_WIDTH
    o_dq = o_nv + NA_WIDTH
    o_dk = o_dq + DIFF_QK_WIDTH
    o_dv = o_dk + DIFF_QK_WIDTH
    o_mq = o_dv + DIFF_V_WIDTH

    for l in range(DEPTH):
        x = x + 0.5 * swiglu(rmsnorm(x, ffn1_norm[l]), ffn1_w_gate[l], ffn1_w_up[l], ffn1_w_down[l])

        h = rmsnorm(x, mix_norm[l])
        proj = h @ w_in[l]

        na_shape = (b, s, NA_HEADS, NA_HEAD_DIM)
        na_q = proj[..., o_nq:o_nk].reshape(na_shape)
        na_k = proj[..., o_nk:o_nv].reshape(na_shape)
        na_v = proj[..., o_nv:o_dq].reshape(na_shape)
        y_na = neighbourhood_attention(na_q, na_k, na_v, na_rpb[l]) @ w_br_na[l]

        dq = proj[..., o_dq:o_dk].reshape(b, s, DIFF_HEADS, 2, DIFF_QK_DIM).transpose(3, 0, 2, 1, 4)
        dk = proj[..., o_dk:o_dv].reshape(b, s, DIFF_HEADS, 2, DIFF_QK_DIM).transpose(3, 0, 2, 1, 4)
        dv = proj[..., o_dv:o_mq].reshape(b, s, DIFF_HEADS, DIFF_V_DIM).transpose(0, 2, 1, 3)
        lam_init = 0.8 - 0.6 * math.exp(-0.3 * l)
        lam = (jnp.exp(jnp.sum(diff_lambda_q1[l].astype(jnp.float32) * diff_lambda_k1[l].astype(jnp.float32)))
               - jnp.exp(jnp.sum(diff_lambda_q2[l].astype(jnp.float32) * diff_lambda_k2[l].astype(jnp.float32)))
               + lam_init)
        o_diff = differential_attention(dq[0], dq[1], dk[0], dk[1], dv, lam, slopes)
        o_diff = rmsnorm(o_diff, diff_subln[l]) * (1.0 - lam_init)
        o_diff = o_diff.transpose(0, 2, 1, 3).reshape(b, s, DIFF_V_WIDTH)
        y_diff = o_diff @ w_br_diff[l]

        mq = proj[..., o_mq:].reshape(b, s, MEM_HEADS, MEM_HEAD_DIM)
        mkv = rmsnorm(mem, mem_norm[l]) @ w_mem_kv[l]
        mk = mkv[..., :MEM_WIDTH].reshape(b, m, MEM_HEADS, MEM_HEAD_DIM)
        mv = mkv[..., MEM_WIDTH:].reshape(b, m, MEM_HEADS, MEM_HEAD_DIM)
        y_mem = memory_attention(mq, mk, mv) @ w_br_mem[l]

        g = jax.nn.sigmoid((h @ w_gate[l] + b_gate[l]).astype(jnp.float32)).astype(h.dtype)
        g = g.reshape(b, s, N_BRANCH, d_model)
        merged = g[:, :, 0] * y_na + g[:, :, 1] * y_diff + g[:, :, 2] * y_mem
        x = x + merged @ w_out[l]

        x = x + 0.5 * swiglu(rmsnorm(x, ffn2_norm[l]), ffn2_w_gate[l], ffn2_w_up[l], ffn2_w_down[l])

    return rmsnorm(x, final_norm)
```

```python
import contextlib
import numpy as np
import concourse.bass as bass
import concourse.mybir as mybir
from concourse.bass_utils import run_bass_kernel_spmd

F32 = mybir.dt.float32
BF16 = mybir.dt.bfloat16
AF = mybir.ActivationFunctionType
ALU = mybir.AluOpType

S = 2048
D = 1024
NT = 16
KD = 8
DFF = 2816
NCH = 22
MEMT = 256
EPS = 1e-6
LAM_INIT = 0.2
SLOPES = [2.0 ** (-8.0 * (i + 1) / 4) for i in range(4)]
O_NQ, O_NK, O_NV, O_DQ, O_DK, O_DV, O_MQ = 0, 512, 1024, 1536, 2048, 2560, 3072
ALW = 3968
MASKV = -30000.0
SAME_ENGINE_SYNC = True
ATTACH_WAIT = True

ENGINES = ("pe", "act", "dve", "pool", "sp")


class Op:
    __slots__ = ("idx", "eng", "fn", "reads", "writes", "is_dma", "stream", "deps",
                 "signal", "count", "deps_x")

    def __init__(self, idx, eng, fn, reads, writes, is_dma, stream):
        self.idx = idx
        self.eng = eng
        self.fn = fn
        self.reads = reads
        self.writes = writes
        self.is_dma = is_dma
        self.stream = stream
        self.deps = set()
        self.deps_x = set()
        self.signal = False
        self.count = 0


class Prog:
    def __init__(self, nc, same_engine_sync=True):
        self.nc = nc
        self.ops = []
        self.last_writer = {}
        self.readers = {}
        self.same_engine_sync = same_engine_sync
        self.stream_names = []
        self.last_of_eng = {}
        self.last_of_stream = {}
        self.pending = {}

    def fence(self):
        snap = set(self.last_of_eng.values()) | set(self.last_of_stream.values())
        for e in ENGINES:
            self.pending[e] = set(snap) | self.pending.get(e, set())

    def _add(self, eng, fn, reads, writes, is_dma, stream):
        op = Op(len(self.ops), eng, fn, tuple(reads), tuple(writes), is_dma, stream)
        deps = set()
        deps_x = set()
        for r in op.reads:
            lw = self.last_writer.get(r)
            if lw is not None:
                deps.add(lw)
        for w in op.writes:
            tgt = deps_x if (isinstance(w, tuple) and w[0] == "PS") else deps
            lw = self.last_writer.get(w)
            if lw is not None:
                tgt.add(lw)
            for rd in self.readers.get(w, ()):
                tgt.add(rd)
        pend = self.pending.pop(eng, None)
        if pend:
            deps_x |= pend
        deps.discard(op.idx)
        deps_x.discard(op.idx)
        op.deps = deps
        op.deps_x = deps_x - deps
        for r in op.reads:
            self.readers.setdefault(r, []).append(op.idx)
        for w in op.writes:
            self.last_writer[w] = op.idx
            self.readers[w] = []
        self.ops.append(op)
        if is_dma:
            self.last_of_stream[stream] = op.idx
        else:
            self.last_of_eng[eng] = op.idx
        return op

    def add(self, eng, fn, reads=(), writes=()):
        return self._add(eng, fn, reads, writes, False, None)

    def dma(self, eng, fn, reads=(), writes=(), stream=None):
        if stream not in self.stream_names:
            self.stream_names.append(stream)
        return self._add(eng, fn, reads, writes, True, stream)

    def emit(self, final_wait_streams=()):
        nc = self.nc
        ops = self.ops

        def skip_dep(op, dop, is_x):
            if dop.is_dma or op.is_dma:
                return False
            if dop.eng != op.eng:
                return False
            if is_x or dop.eng == "pe" or not self.same_engine_sync:
                return True
            return False

        for op in ops:
            latest = {}
            for dset, is_x in ((op.deps, False), (op.deps_x, True)):
                for d in dset:
                    dop = ops[d]
                    if not dop.is_dma and not skip_dep(op, dop, is_x):
                        if latest.get(dop.eng, -1) < d:
                            latest[dop.eng] = d
            op.deps = set(d for d in op.deps if ops[d].is_dma)
            op.deps_x = set(d for d in op.deps_x if ops[d].is_dma)
            for d in latest.values():
                ops[d].signal = True
                op.deps_x.add(d)
        self_skip = skip_dep

        def skip_dep(op, dop, is_x):
            return False
        eng_count = {e: 0 for e in ENGINES}
        stream_count = {s: 0 for s in self.stream_names}
        stream_hist = {s: [] for s in self.stream_names}
        for op in ops:
            if op.is_dma:
                stream_count[op.stream] += 16
                op.count = stream_count[op.stream]
                stream_hist[op.stream].append((op.idx, op.count))
            elif op.signal:
                eng_count[op.eng] += 1
                op.count = eng_count[op.eng]
        with contextlib.ExitStack() as es:
            eng_sem = {e: es.enter_context(nc.semaphore("s_" + e)) for e in ENGINES}
            st_sem = {s: es.enter_context(nc.semaphore("d_" + str(i)))
                      for i, s in enumerate(self.stream_names)}
            block = es.enter_context(nc.Block())
            per_eng = {e: [op for op in ops if op.eng == e] for e in ENGINES}

            def stream_value_before(stream, idx):
                v = 0
                for (i, c) in stream_hist[stream]:
                    if i < idx:
                        v = c
                    else:
                        break
                return v

            vc = [None] * len(ops)
            know = {e: {} for e in ENGINES}
            last_sig = {e: 0 for e in ENGINES}
            waits_of = [None] * len(ops)

            def merge_into(dst, src):
                for k, v in src.items():
                    if dst.get(k, 0) < v:
                        dst[k] = v

            for op in ops:
                K = know[op.eng]
                need = {}
                for d in list(op.deps) + list(op.deps_x):
                    dop = ops[d]
                    if dop.is_dma:
                        key = ("d", dop.stream)
                        val = stream_value_before(dop.stream, op.idx)
                    else:
                        key = ("e", dop.eng)
                        val = dop.count
                    if need.get(key, (0, None))[0] < val:
                        need[key] = (val, d)
                wl = []
                for key, (val, d) in sorted(need.items(), key=lambda kv: -kv[1][1]):
                    if K.get(key, 0) >= val:
                        continue
                    wl.append((key, val))
                    if K.get(key, 0) < val:
                        K[key] = val
                    merge_into(K, vc[d])
                waits_of[op.idx] = wl
                v = dict(K)
                if op.is_dma:
                    v[("d", op.stream)] = max(v.get(("d", op.stream), 0), op.count)
                else:
                    if op.signal:
                        last_sig[op.eng] = op.count
                    v[("e", op.eng)] = max(v.get(("e", op.eng), 0), last_sig[op.eng])
                    if not self.same_engine_sync or op.eng == "pe":
                        pass
                vc[op.idx] = v
            self.n_waits = sum(len(w) for w in waits_of)

            def run(engname, eng):
                for op in per_eng[engname]:
                    wl = waits_of[op.idx]
                    attach = None
                    if ATTACH_WAIT and wl and not op.is_dma:
                        attach = wl[-1]
                        wl = wl[:-1]
                    for key, val in wl:
                        sem = st_sem[key[1]] if key[0] == "d" else eng_sem[key[1]]
                        eng.wait_ge(sem, val)
                    ins = op.fn(eng)
                    if attach is not None:
                        key, val = attach
                        sem = st_sem[key[1]] if key[0] == "d" else eng_sem[key[1]]
                        ins._wait_ge(sem, val)
                    if op.is_dma:
                        ins.then_inc(st_sem[op.stream], 16)
                    elif op.signal:
                        ins.then_inc(eng_sem[op.eng], 1)
                if engname == "sp":
                    for s in final_wait_streams:
                        eng.wait_ge(st_sem[s], stream_count[s])

            block.tensor(lambda e: run("pe", e))
            block.scalar(lambda e: run("act", e))
            block.vector(lambda e: run("dve", e))
            block.gpsimd(lambda e: run("pool", e))
            block.sync(lambda e: run("sp", e))
        return eng_count, stream_count


class Rot:
    def __init__(self, items):
        self.items = items
        self.i = 0

    def next(self):
        it = self.items[self.i % len(self.items)]
        self.i += 1
        return it


def na_key_tiles(t):
    if t <= 1:
        return [(kt, 5 + 4 * t + kt) for kt in range(4)]
    if t >= 14:
        return [(kt, 13 + 4 * (t - 14) + (kt - 12)) for kt in range(12, 16)]
    return [(kt, kt - t + 2) for kt in range(t - 2, t + 3)]


def na_variant_reps():
    reps = {}
    for t in (5, 0, 1, 14, 15):
        for kt, v in na_key_tiles(t):
            reps[v] = (t, kt)
    return [reps[v] for v in range(21)]


def build_nab(rpb):
    out = np.empty((8, 21, 128, 128), np.float32)
    idx = np.arange(128)
    for v, (t, kt) in enumerate(na_variant_reps()):
        qr = 2 * t + idx // 64
        qc = idx % 64
        kr = 2 * kt + idx // 64
        kc = idx % 64
        rs = np.clip(qr - 4, 0, 24)
        cs = np.clip(qc - 8, 0, 48)
        KR, QR = kr[:, None], qr[None, :]
        KC, QC = kc[:, None], qc[None, :]
        inside = (KR >= rs[None, :]) & (KR < rs[None, :] + 8) & (KC >= cs[None, :]) & (KC < cs[None, :] + 16)
        dr = np.clip(KR - QR + 7, 0, 14)
        dc = np.clip(KC - QC + 15, 0, 30)
        g = rpb[:, dr, dc]
        out[:, v] = np.where(inside[None], g, np.float32(MASKV))
    return out


def build_alibi():
    k = np.arange(128, dtype=np.float64)[:, None]
    u = np.arange(ALW, dtype=np.float64)[None, :] - 1920.0
    dist = np.abs(u - k)
    return np.stack([(-s * dist).astype(np.float32) for s in SLOPES])


def build_nc():
    nc = bass.Bass("TRN2", target_bir_lowering=False)

    def din(name, shape):
        return nc.dram_tensor(name, list(shape), F32, kind="ExternalInput").ap()

    x_d = din("x", [S, D])
    mem_d = din("mem", [MEMT, D])
    w1g, w1u, w1d = din("ffn1_w_gate", [D, DFF]), din("ffn1_w_up", [D, DFF]), din("ffn1_w_down", [DFF, D])
    w2g, w2u, w2d = din("ffn2_w_gate", [D, DFF]), din("ffn2_w_up", [D, DFF]), din("ffn2_w_down", [DFF, D])
    win_d = din("w_in", [D, 3584])
    wkv_d = din("w_mem_kv", [D, 1024])
    wgate_d = din("w_gate", [D, 3072])
    wbr_d = [din("w_br_mem", [512, D]), din("w_br_na", [512, D]), din("w_br_diff", [512, D])]
    wout_d = din("w_out", [D, D])
    NPAR = 441
    par_d = din("params", [128, NPAR])
    gfin_d = din("gfin", [128, D])
    ident_d = din("ident", [128, 128])
    alibi_d = din("alibi", [4, 128, ALW])
    nab_d = din("nab", [4, 128, 2 * 21 * 128])
    y_d = nc.dram_tensor("y", [S, D], F32, kind="ExternalOutput").ap()

    xs = nc.alloc_sbuf_tensor("xs", [128, NT, D], F32)
    hT = nc.alloc_sbuf_tensor("hT", [128, KD, S], BF16)
    par = nc.alloc_sbuf_tensor("par", [128, NPAR], F32)
    identb = nc.alloc_sbuf_tensor("identb", [128, 128], BF16)
    stat = nc.alloc_sbuf_tensor("stat", [128, 64], F32)
    RB = 110592
    R = nc.alloc_sbuf_tensor("R", [128, RB // 4], F32)

    def rv(off, nbytes, dt, pat=None, **kw):
        assert off % 4 == 0 and nbytes % 4 == 0 and off + nbytes <= RB
        v = R[:, off // 4:(off + nbytes) // 4]
        if dt is not F32:
            v = v.bitcast(dt)
        if pat:
            v = v.rearrange(pat, **kw)
        return v

    banks = [nc.alloc_psum_tensor("B%d" % i, [128, 512], F32) for i in range(8)]

    def bk(i):
        return banks[i][:, :], ("PS", i)

    def bkb(i):
        return banks[i][:, :].bitcast(BF16), ("PS", i)

    P = Prog(nc, same_engine_sync=SAME_ENGINE_SYNC)

    P.dma("sp", lambda e: e.dma_start(out=par[:], in_=par_d[:, :]), writes=["par"], stream="par")
    for i in range(4):
        P.dma("sp", lambda e, i=i: e.dma_start(
            out=xs[:, 4 * i:4 * i + 4, :],
            in_=x_d[512 * i:512 * (i + 1), :].rearrange("(t p) d -> p t d", p=128)),
            writes=[("xs", t) for t in range(4 * i, 4 * i + 4)], stream="x%d" % i)
    P.dma("pool", lambda e: e.dma_start(out=identb[:], in_=ident_d[:, :]), writes=["identb"], stream="ident")

    def wload(dst, src2d, key, stream, pat="(k p) n -> p k n", alias=()):
        P.dma("pool", lambda e: e.dma_start(out=dst, in_=src2d.rearrange(pat, p=128)),
              writes=[key] + list(alias), stream=stream)

    rstd_ctr = [0]

    def rstd_ops(ss_ap, out_ap, n, rkeys, wkey, width):
        assert width <= 4
        c0 = 48 + (rstd_ctr[0] % 4) * 4
        rstd_ctr[0] += 1
        tmp = stat[:, c0:c0 + width]
        tk = ("stat_tmp", c0)
        P.add("act", lambda e: e.activation(out=tmp, in_=ss_ap, func=AF.Ln, scale=1.0 / n, bias=epsb[:, 0:1]),
              reads=list(rkeys) + ["epsb"], writes=[tk])
        P.add("act", lambda e: e.activation(out=out_ap, in_=tmp, func=AF.Exp, scale=-0.5),
              reads=[tk], writes=[wkey])

    epsb = nc.alloc_sbuf_tensor("epsb", [128, 1], F32)
    P.add("dve", lambda e: e.memset(epsb[:], EPS), writes=["epsb"])

    ss = stat[:, 0:16]
    rstd = stat[:, 16:32]

    def norm_stats(g, junk):
        for t in range(4 * g, 4 * g + 4):
            P.add("act", lambda e, t=t: e.activation(out=junk, in_=xs[:, t, :], func=AF.Square,
                                                      accum_out=ss[:, t:t + 1]),
                  reads=[("xs", t)], writes=["junk", ("ss", t)])
        rstd_ops(ss[:, 4 * g:4 * g + 4], rstd[:, 4 * g:4 * g + 4], D, [("ss", u) for u in range(4 * g, 4 * g + 4)],
                 ("rstd", g), 4)

    def norm_emit(g, gcol, hb, trpool):
        for t in range(4 * g, 4 * g + 4):
            hbt = hb[t % 2]
            hk = ("hb", t % 2)
            P.add("act", lambda e, t=t, hbt=hbt: e.activation(
                out=hbt, in_=xs[:, t, :], func=AF.Copy, scale=rstd[:, t:t + 1]),
                reads=[("xs", t), ("rstd", t // 4)], writes=[hk])
            pb, pk = trpool.next()
            for k in range(KD):
                P.add("pe", lambda e, k=k, hbt=hbt, pb=pb: e.transpose(
                    out=pb[:, k * 128:(k + 1) * 128], in_=hbt[:, k * 128:(k + 1) * 128], identity=identb[:]),
                    reads=[hk, "identb"], writes=[pk])
            P.add("dve", lambda e, t=t, pb=pb: e.tensor_tensor(
                out=hT[:, :, t * 128:(t + 1) * 128],
                in0=pb[:, 0:1024].rearrange("p (k n) -> p k n", k=KD),
                in1=par[:, gcol:gcol + KD].unsqueeze(2).to_broadcast([128, KD, 128]),
                op=ALU.mult),
                reads=["par"], writes=[("hT", t), pk])

    def norm_to_hT(gcol, hb, junk, trpool):
        for g in range(4):
            norm_stats(g, junk)
        for g in range(4):
            norm_emit(g, gcol, hb, trpool)

    WG_ALIAS = {0: [("mT", c, n) for c in (6, 7) for n in range(4)],
                1: [("oT", k, n) for k in (0, 1) for n in range(4)],
                2: [("oT", k, n) for k in (2, 3) for n in range(4)]}
    WD_ALIAS = {0: ["wout", ("spr", 2)] + [("qT", n) for n in range(4)] + [("qB", n) for n in range(4)],
                1: ["wout", ("wsl", 0, 0), ("wsl", 0, 1), ("wsl", 0, 2)]}

    def ffn(wg_d, wu_d, wd_d, tag, hook_a=None, hook_b=None, hook_c=None, pidx0=0, preloaded=0, pre_unit=None):
        actT = rv(0, 24576, BF16, "p (c n) -> p c n", c=6)
        wgs = [rv(24576 + i * 8192, 4096, BF16, "p (k n) -> p k n", k=KD) for i in range(3)]
        wus = [rv(24576 + i * 8192 + 4096, 4096, BF16, "p (k n) -> p k n", k=KD) for i in range(3)]
        wds = [rv(49152 + i * 12288, 12288, BF16, "p (c n) -> p c n", c=6) for i in range(2)]
        sgs = [rv(77824 + i * 2048, 2048, F32) for i in range(2)]
        pgp = Rot([bk(0), bk(1)])
        pup = Rot([bk(2), bk(3)])
        pop = Rot([bk(4), bk(5)])
        groups = [[0, 1, 2], [3, 4, 5], [6, 7, 8], [9, 10]]
        pidx = pidx0
        npl = 0
        ev = 0
        for gi, grp in enumerate(groups):
            ncg = 2 * len(grp)
            c0 = 2 * grp[0]
            wd = wds[gi % 2]
            for pi in grp:
                sl = pidx % 3
                pidx += 1
                if npl >= preloaded:
                    wload(wgs[sl], wg_d[:, pi * 256:(pi + 1) * 256], ("wg", sl), "wgu%d" % sl, alias=WG_ALIAS[sl])
                    wload(wus[sl], wu_d[:, pi * 256:(pi + 1) * 256], ("wu", sl), "wgu%d" % sl, alias=WG_ALIAS[sl])
                npl += 1
                if pi == grp[0]:
                    wload(wd[:, 0:ncg, :], wd_d[c0 * 128:(c0 + ncg) * 128, :], ("wd", gi % 2), "wd%d" % (gi % 2),
                          pat="(c p) n -> p c n", alias=WD_ALIAS[gi % 2])
                for cc in range(2):
                    cl = (pi - grp[0]) * 2 + cc
                    for n in range(4):
                        if pre_unit is not None and gi == 0 and pi == grp[0] and cc == 0:
                            pre_unit(n)
                        pg, pgk = pgp.next()
                        pu, puk = pup.next()
                        hkeys = [("hT", t) for t in range(4 * n, 4 * n + 4)]
                        for k in range(KD):
                            P.add("pe", lambda e, k=k, pg=pg, sl=sl, cc=cc, n=n: e.matmul(
                                pg, lhsT=wgs[sl][:, k, cc * 128:(cc + 1) * 128],
                                rhs=hT[:, k, n * 512:(n + 1) * 512], start=(k == 0), stop=(k == KD - 1)),
                                reads=[("wg", sl)] + hkeys, writes=[pgk])
                        for k in range(KD):
                            P.add("pe", lambda e, k=k, pu=pu, sl=sl, cc=cc, n=n: e.matmul(
                                pu, lhsT=wus[sl][:, k, cc * 128:(cc + 1) * 128],
                                rhs=hT[:, k, n * 512:(n + 1) * 512], start=(k == 0), stop=(k == KD - 1)),
                                reads=[("wu", sl)] + hkeys, writes=[puk])
                        sg = sgs[ev % 2]
                        sgk = ("sg", ev % 2)
                        ev += 1
                        P.add("act", lambda e, sg=sg, pg=pg: e.activation(out=sg, in_=pg, func=AF.Silu),
                              writes=[sgk, pgk])
                        P.add("dve", lambda e, sg=sg, pu=pu, cl=cl, n=n: e.tensor_tensor(
                            out=actT[:, cl, n * 512:(n + 1) * 512], in0=pu, in1=sg, op=ALU.mult),
                            reads=[sgk], writes=[("actT", cl, n), puk])
                if hook_a is not None and gi == 0 and pi == grp[0]:
                    hook_a()
            if hook_b is not None and gi == 0:
                hook_b()
            for t in range(NT):
                for j in range(2):
                    po, pok = pop.next()
                    for c in range(ncg):
                        P.add("pe", lambda e, c=c, po=po, t=t, j=j, wd=wd, ncg=ncg: e.matmul(
                            po, lhsT=actT[:, c, t * 128:(t + 1) * 128], rhs=wd[:, c, j * 512:(j + 1) * 512],
                            start=(c == 0), stop=(c == ncg - 1)),
                            reads=[("actT", c, t // 4), ("wd", gi % 2)], writes=[pok])
                    P.add("dve", lambda e, po=po, t=t, j=j: e.scalar_tensor_tensor(
                        out=xs[:, t, j * 512:(j + 1) * 512], in0=po, scalar=0.5,
                        in1=xs[:, t, j * 512:(j + 1) * 512], op0=ALU.mult, op1=ALU.add),
                        writes=[("xs", t), pok])
            if hook_c is not None and gi == 0:
                hook_c()


    memraw = rv(83968, 8192, F32, "p (t d) -> p t d", t=2)
    memnb = rv(92160, 4096, BF16, "p (t d) -> p t d", t=2)
    memnT = rv(96256, 4096, BF16, "p (k n) -> p k n", k=KD)
    wkc = [rv(100352, 4096, BF16, "p (k n) -> p k n", k=KD),
           rv(83968, 4096, BF16, "p (k n) -> p k n", k=KD),
           rv(88064, 4096, BF16, "p (k n) -> p k n", k=KD),
           rv(92160, 4096, BF16, "p (k n) -> p k n", k=KD)]
    mv = rv(104448, 2112, BF16, "p (t h c) -> p t h c", t=2, h=4)
    mkT = rv(106560, 2048, BF16, "p (h n) -> p h n", h=4)
    ssm = stat[:, 38:40]
    rsm = stat[:, 40:42]

    def mem_prep_a():
        P.dma("sp", lambda e: e.dma_start(out=memraw, in_=mem_d.rearrange("(t p) d -> p t d", p=128)),
              writes=["memraw"], stream="mem")
        wload(wkc[0], wkv_d[:, 0:256], ("wkc", 0), "wkc0")
        for t in range(2):
            P.add("act", lambda e, t=t: e.activation(out=junk_f, in_=memraw[:, t, :], func=AF.Square,
                                                      accum_out=ssm[:, t:t + 1]),
                  reads=["memraw"], writes=["junk", ("ssm", t)])
        rstd_ops(ssm, rsm, D, [("ssm", 0), ("ssm", 1)], "rsm", 2)
        for t in range(2):
            P.add("dve", lambda e, t=t: e.tensor_scalar(out=memnb[:, t, :], in0=memraw[:, t, :],
                                                        scalar1=rsm[:, t:t + 1], scalar2=None, op0=ALU.mult),
                  reads=["memraw", "rsm"], writes=[("memnb", t)])

    def mem_prep_b():
        P.dma("pool", lambda e: e.dma_start(out=wkc[1], in_=wkv_d[:, 256:512].rearrange("(k p) n -> p k n", p=128)),
              writes=[("wkc", 1), "memraw"], stream="wkc1")
        P.dma("pool", lambda e: e.dma_start(out=wkc[2], in_=wkv_d[:, 512:768].rearrange("(k p) n -> p k n", p=128)),
              writes=[("wkc", 2), "memraw"], stream="wkc2")
        for t in range(2):
            pb, pk = bkb(6 + t)
            for k in range(KD):
                P.add("pe", lambda e, k=k, t=t, pb=pb: e.transpose(
                    out=pb[:, k * 128:(k + 1) * 128], in_=memnb[:, t, k * 128:(k + 1) * 128], identity=identb[:]),
                    reads=[("memnb", t), "identb"], writes=[pk])
            P.add("dve", lambda e, t=t, pb=pb: e.tensor_tensor(
                out=memnT[:, :, t * 128:(t + 1) * 128],
                in0=pb[:, 0:1024].rearrange("p (k n) -> p k n", k=KD),
                in1=par[:, 24:32].unsqueeze(2).to_broadcast([128, KD, 128]), op=ALU.mult),
                reads=["par"], writes=["memnT", pk])
        P.dma("pool", lambda e: e.dma_start(out=wkc[3], in_=wkv_d[:, 768:1024].rearrange("(k p) n -> p k n", p=128)),
              writes=[("wkc", 3), ("memnb", 0), ("memnb", 1)], stream="wkc3")

    def mem_prep_c():
        P.add("dve", lambda e: e.memset(mv[:, :, :, 128:129], 1.0), writes=["mv"])
        mpool = Rot([bk(6), bk(7)])
        for ci in range(4):
            sl = ci
            if ci < 2:
                for hh in range(2):
                    h = 2 * ci + hh
                    pb, pk = mpool.next()
                    for k in range(KD):
                        P.add("pe", lambda e, k=k, hh=hh, pb=pb, sl=sl: e.matmul(
                            pb[:, 0:256], lhsT=wkc[sl][:, k, hh * 128:(hh + 1) * 128], rhs=memnT[:, k, :],
                            start=(k == 0), stop=(k == KD - 1)),
                            reads=[("wkc", sl), "memnT"], writes=[pk])
                    P.add("act", lambda e, h=h, pb=pb: e.activation(out=mkT[:, h, :], in_=pb[:, 0:256], func=AF.Copy),
                          writes=["mkT", pk])
            else:
                for t in range(2):
                    pb, pk = mpool.next()
                    for k in range(KD):
                        P.add("pe", lambda e, k=k, t=t, pb=pb, sl=sl: e.matmul(
                            pb[:, 0:256], lhsT=memnT[:, k, t * 128:(t + 1) * 128], rhs=wkc[sl][:, k, :],
                            start=(k == 0), stop=(k == KD - 1)),
                            reads=[("wkc", sl), "memnT"], writes=[pk])
                    h0 = 2 * (ci - 2)
                    P.add("dve", lambda e, t=t, pb=pb, h0=h0: e.tensor_copy(
                        out=mv[:, t, h0:h0 + 2, 0:128], in_=pb[:, 0:256].rearrange("p (h c) -> p h c", h=2)),
                        reads=["mv"], writes=["mv2", pk])

    hb_f = [rv(73728, 2048, BF16), rv(75776, 2048, BF16)]
    junk_f = rv(81920, 2048, BF16)
    trp = Rot([bkb(6), bkb(7)])

    for g in range(4):
        norm_stats(g, junk_f)
    ffn(w1g, w1u, w1d, "f1", hook_a=mem_prep_a, hook_b=mem_prep_b, hook_c=mem_prep_c,
        pre_unit=lambda n: norm_emit(n, 0, hb_f, trp))
    norm_to_hT(8, hb_f, junk_f, Rot([bkb(6), bkb(7)]))
    P.fence()

    mergedT = rv(0, 32768, BF16, "p (k n) -> p k n", k=KD)
    oT = rv(32768, 16384, BF16, "p (k n) -> p k n", k=4)
    hb_m = [rv(49152, 2048, BF16), rv(51200, 2048, BF16)]
    junk_m = rv(53248, 2048, BF16)
    SB = 55296
    qT = rv(SB, 4096, BF16)
    kT = rv(SB + 4096, 4096, BF16)
    vaug = rv(SB + 8192, 4224, BF16, "p (t c) -> p t c", t=NT)
    wsl = [[rv(SB + 12416 + s * 6144 + i * 2048, 2048, BF16, "p (k n) -> p k n", k=KD) for i in range(3)]
           for s in range(2)]
    Et = [rv(SB + 24704 + i * 1024, 1024, BF16) for i in range(4)]
    otok = rv(SB + 28800, 1024, F32)
    otokb = [rv(SB + 29824 + i * 256, 256, BF16) for i in range(4)]
    spr = [rv(SB + 30848 + i * 2048, 2048, F32) for i in range(2)]
    XO = SB + 34944
    alib = rv(XO, 15872, F32)
    nabt = rv(XO, 10752, BF16, "p (h v q) -> p h v q", h=2, v=21)
    nabt_flat = rv(XO, 10752, BF16)

    poolA = Rot([bk(0), bk(1), bk(2)])
    poolB = Rot([bk(3), bk(4), bk(5)])
    poolC = Rot([bk(6), bk(7)])


    lamt = stat[:, 32:36]
    P.add("dve", lambda e: e.tensor_tensor(out=otok[:, 0:64], in0=par[:, 56:120], in1=par[:, 120:184], op=ALU.mult),
          reads=["par"], writes=["otok"])
    P.add("dve", lambda e: e.reduce_sum(out=lamt[:, 0:1], in_=otok[:, 0:64], axis=mybir.AxisListType.X),
          reads=["otok"], writes=["lam0"])
    P.add("dve", lambda e: e.tensor_tensor(out=otok[:, 64:128], in0=par[:, 184:248], in1=par[:, 248:312], op=ALU.mult),
          reads=["par"], writes=["otok2"])
    P.add("dve", lambda e: e.reduce_sum(out=lamt[:, 1:2], in_=otok[:, 64:128], axis=mybir.AxisListType.X),
          reads=["otok2"], writes=["lam1"])
    P.add("act", lambda e: e.activation(out=lamt[:, 2:4], in_=lamt[:, 0:2], func=AF.Exp),
          reads=["lam0", "lam1"], writes=["lam2"])
    nlam = stat[:, 36:37]
    P.add("dve", lambda e: e.scalar_tensor_tensor(out=nlam, in0=lamt[:, 3:4], scalar=-LAM_INIT, in1=lamt[:, 2:3],
                                                  op0=ALU.add, op1=ALU.subtract),
          reads=["lam2"], writes=["nlam"])
    QB = rv(49152, 4096, BF16)
    QAB = [qT, QB]

    def proj_fm(dst, col0, slot_ap, wkey, scale, split=False):
        for n in range(4):
            pb, pk = poolC.next()
            for k in range(KD):
                P.add("pe", lambda e, k=k, n=n, pb=pb: e.matmul(
                    pb, lhsT=slot_ap[:, k, :], rhs=hT[:, k, n * 512:(n + 1) * 512],
                    start=(k == 0), stop=(k == KD - 1)),
                    reads=[wkey] + [("hT", t) for t in range(4 * n, 4 * n + 4)], writes=[pk])
            if dst is qT and split:
                P.add("act", lambda e, n=n, pb=pb: e.activation(out=qT[0:64, n * 512:(n + 1) * 512], in_=pb[0:64, :],
                                                                func=AF.Copy, scale=scale),
                      writes=[("qT", n), pk])
                P.add("act", lambda e, n=n, pb=pb: e.activation(out=QB[64:128, n * 512:(n + 1) * 512],
                                                                in_=pb[64:128, :], func=AF.Copy, scale=scale),
                      writes=[("qT", n), pk])
            else:
                P.add("act", lambda e, n=n, pb=pb: e.activation(out=dst[:, n * 512:(n + 1) * 512], in_=pb,
                                                                func=AF.Copy, scale=scale),
                      writes=[(dst_key[id(dst)], n), pk])

    dst_key = {id(qT): "qT", id(kT): "kT", id(QB): "qB"}

    def transposes_to_oT(chunk, n, srcs, skeys):
        pb, pk = poolC.next()
        pbb = pb.bitcast(BF16)
        for j in range(4):
            P.add("pe", lambda e, j=j, pbb=pbb: e.transpose(out=pbb[:, j * 128:(j + 1) * 128], in_=srcs[j],
                                                            identity=identb[:]),
                  reads=[skeys[j], "identb"], writes=[pk])
        P.add("act", lambda e, pbb=pbb: e.activation(out=oT[:, chunk, n * 512:(n + 1) * 512], in_=pbb[:, 0:512],
                                                     func=AF.Copy),
              writes=[("oT", chunk, n), pk])

    poolS = Rot([bk(3), bk(4), bk(5), bk(6), bk(7)])

    def pipeline(units, stage_a, stage_b, look, with_index=False):
        nU = len(units)
        for i in range(nU + look):
            if i < nU:
                stage_a(units[i], i) if with_index else stage_a(units[i])
            if i >= look:
                stage_b(units[i - look], i - look) if with_index else stage_b(units[i - look])

    wctr = [0]

    def load_win(cols, force=None):
        s = wctr[0] % 2 if force is None else force
        wctr[0] += 1
        for i, c0 in enumerate(cols):
            wload(wsl[s][i], win_d[:, c0:c0 + 128], ("wsl", s, i), "wsl%d" % s)
        return s

    def merge(bi, wbr, first, after_first=None):
        wbrs = [rv(SB + 18560 + i * 1024, 1024, BF16, "p (k n) -> p k n", k=4) for i in range(2)]
        wgts = [rv(SB + 20608 + i * 2048, 2048, BF16, "p (k n) -> p k n", k=KD) for i in range(2)]
        sigs = [spr[i] for i in range(2)]
        tts = [rv(SB + 24704 + i * 2048, 2048, F32) for i in range(2)]
        gidx = {0: 2, 1: 0, 2: 1}[bi]
        ev = 0
        for c in range(KD):
            sl = c % 2
            wload(wbrs[sl], wbr[:, c * 128:(c + 1) * 128], ("wbr", sl), "wm%d" % sl)
            wload(wgts[sl], wgate_d[:, gidx * 1024 + c * 128:gidx * 1024 + (c + 1) * 128], ("wgt", sl), "wm%d" % sl)
            if c == 0 and after_first is not None:
                after_first()
            for n in range(4):
                py, pyk = poolA.next()
                pg, pgk = poolB.next()
                for k in range(4):
                    P.add("pe", lambda e, k=k, py=py, sl=sl, n=n: e.matmul(
                        py, lhsT=wbrs[sl][:, k, :], rhs=oT[:, k, n * 512:(n + 1) * 512],
                        start=(k == 0), stop=(k == 3)),
                        reads=[("wbr", sl), ("oT", k, n)], writes=[pyk])
                for k in range(KD):
                    P.add("pe", lambda e, k=k, pg=pg, sl=sl, n=n: e.matmul(
                        pg, lhsT=wgts[sl][:, k, :], rhs=hT[:, k, n * 512:(n + 1) * 512],
                        start=(k == 0), stop=(k == KD - 1)),
                        reads=[("wgt", sl)] + [("hT", t) for t in range(4 * n, 4 * n + 4)], writes=[pgk])
                sg = sigs[ev % 2]
                tt = tts[ev % 2]
                sgk, ttk = ("spr", ev % 2), ("tt", ev % 2)
                tal = [("E", 2 * (ev % 2)), ("E", 2 * (ev % 2) + 1)]
                ev += 1
                bcol = 32 + gidx * 8 + c
                P.add("act", lambda e, sg=sg, pg=pg, bcol=bcol: e.activation(
                    out=sg, in_=pg, func=AF.Sigmoid, bias=par[:, bcol:bcol + 1]),
                    reads=["par"], writes=[sgk, pgk])
                mslice = mergedT[:, c, n * 512:(n + 1) * 512]
                if first:
                    P.add("dve", lambda e, sg=sg, py=py, mslice=mslice: e.tensor_tensor(
                        out=mslice, in0=py, in1=sg, op=ALU.mult),
                        reads=[sgk], writes=[("mT", c, n), pyk])
                else:
                    P.add("dve", lambda e, sg=sg, py=py, tt=tt: e.tensor_tensor(
                        out=tt, in0=py, in1=sg, op=ALU.mult), reads=[sgk], writes=[ttk, pyk] + tal)
                    P.add("dve", lambda e, tt=tt, mslice=mslice: e.tensor_tensor(
                        out=mslice, in0=tt, in1=mslice, op=ALU.add), reads=[ttk] + tal, writes=[("mT", c, n)])

    for h in range(3):
        wload(wsl[0][h % 3], win_d[:, O_MQ + h * 128:O_MQ + (h + 1) * 128], ("wsl", 0, h % 3), "wsl0")
    wctr[0] += 4
    qbufs = [qT, QB]
    macc = Rot([bk(0), bk(1), bk(2), bk(3)])
    mscore = Rot([bk(4), bk(5), bk(6), bk(7)])
    munits = []
    for h in range(4):
        for n in range(4):
            accb = [macc.next(), macc.next()]
            u = len(munits)
            ets = [(Et[(2 * u + kt) % 4], ("E", (2 * u + kt) % 4)) for kt in range(2)]
            munits.append((h, n, accb, ets))
    proj_fm(qbufs[0], 0, wsl[0][0], ("wsl", 0, 0), 128.0 ** -0.5)
    wload(wsl[0][0], win_d[:, O_MQ + 3 * 128:O_MQ + 4 * 128], ("wsl", 0, 0), "wsl0")

    def m_stage_a(u):
        h, n, accb, ets = u
        qb_ = qbufs[h % 2]
        qk = "qT" if h % 2 == 0 else "qB"
        if n == 1 and h + 1 < 4:
            proj_fm(qbufs[(h + 1) % 2], 0, wsl[0][(h + 1) % 3], ("wsl", 0, (h + 1) % 3), 128.0 ** -0.5)
        for kt in range(2):
            pb, pk = mscore.next()
            E, ek = ets[kt]
            P.add("pe", lambda e, kt=kt, n=n, pb=pb, h=h, qb_=qb_: e.matmul(
                pb, lhsT=mkT[:, h, kt * 128:(kt + 1) * 128], rhs=qb_[:, n * 512:(n + 1) * 512],
                start=True, stop=True), reads=["mkT", (qk, n)], writes=[pk])
            P.add("act", lambda e, E=E, pb=pb: e.activation(out=E, in_=pb, func=AF.Exp),
                  writes=[ek, pk])

    def m_stage_b(u):
        h, n, accb, ets = u
        for j in range(4):
            ab, abk = accb[j // 2]
            a = ab[:, (j % 2) * 132:(j % 2) * 132 + 129]
            for kt in range(2):
                E, ek = ets[kt]
                P.add("pe", lambda e, a=a, E=E, j=j, kt=kt, h=h: e.matmul(
                    a, lhsT=E[:, j * 128:(j + 1) * 128], rhs=mv[:, kt, h, 0:129],
                    start=(kt == 0 and j % 2 == 0), stop=(kt == 1), skip_group_check=True),
                    reads=[ek, "mv2"], writes=[abk])
        for j in range(4):
            ab, abk = accb[j // 2]
            a = ab[:, (j % 2) * 132:(j % 2) * 132 + 129]
            rc = stat[:, 44 + j:45 + j]
            P.add("dve", lambda e, a=a, rc=rc: e.reciprocal(out=rc, in_=a[:, 128:129]),
                  writes=[("rc", j), abk])
            P.add("dve", lambda e, a=a, rc=rc, j=j: e.tensor_scalar(
                out=otokb[j], in0=a[:, 0:128], scalar1=rc, scalar2=None, op0=ALU.mult),
                reads=[("rc", j)], writes=[("otokb", j), abk])
        transposes_to_oT(h, n, otokb, [("otokb", j) for j in range(4)])

    pipeline(munits, m_stage_a, m_stage_b, 1)
    na_pre = {}

    def pre_na():
        na_pre[0] = (load_win([O_NQ, O_NK, O_NV], force=0), True)
        P.dma("pool", lambda e: e.dma_start(out=nabt_flat, in_=nab_d[0]), writes=["nabt"], stream="nab")
    merge(0, wbr_d[0], True, after_first=pre_na)
    vaugn = vaug[:, :, 0:130].rearrange("p t (h c) -> p t h c", h=2)
    P.add("dve", lambda e: e.memset(qT[64:128, :], 0.0), writes=[("qT", n) for n in range(4)])
    P.add("dve", lambda e: e.memset(QB[0:64, :], 0.0), writes=[("qT", n) for n in range(4)] + [("qB", n) for n in range(4)])
    for hp in range(4):
        if hp in na_pre:
            s = na_pre[hp][0]
        else:
            s = load_win([O_NQ + hp * 128, O_NK + hp * 128, O_NV + hp * 128], force=0)
            P.dma("pool", lambda e, hp=hp: e.dma_start(out=nabt_flat, in_=nab_d[hp]),
                  writes=["nabt"], stream="nab")
        proj_fm(qT, 0, wsl[s][0], ("wsl", s, 0), 0.125, split=True)
        proj_fm(kT, 0, wsl[s][1], ("wsl", s, 1), 1.0)
        P.add("dve", lambda e: e.memset(vaugn[:, :, :, 64:65], 1.0), writes=["vones"])
        for g in range(4):
            pb, pk = poolC.next()
            for tt_ in range(4):
                t = 4 * g + tt_
                for k in range(KD):
                    P.add("pe", lambda e, k=k, t=t, tt_=tt_, pb=pb, s=s: e.matmul(
                        pb[:, tt_ * 128:(tt_ + 1) * 128], lhsT=hT[:, k, t * 128:(t + 1) * 128],
                        rhs=wsl[s][2][:, k, :], start=(k == 0), stop=(k == KD - 1)),
                        reads=[("wsl", s, 2), ("hT", t)], writes=[pk])
            P.add("dve", lambda e, g=g, pb=pb: e.tensor_copy(
                out=vaugn[:, 4 * g:4 * g + 4, :, 0:64],
                in_=pb.rearrange("p (t h c) -> p t h c", t=4, h=2)),
                reads=["vones"], writes=[("v", g), pk])
        units = []
        ecnt = 0
        for t in range(NT):
            ab, abk = poolA.next()
            for hh in range(2):
                kts = na_key_tiles(t)
                chunks = [kts[0:4], kts[4:]] if len(kts) > 4 else [kts]
                es = []
                for ch in chunks:
                    es.append((Et[ecnt % 4], ("E", ecnt % 4)))
                    ecnt += 1
                units.append((t, hh, ab, abk, chunks, es))

        def stage_a(u):
            t, hh, ab, abk, chunks, es = u
            for ci, ch in enumerate(chunks):
                pb, pk = poolS.next()
                for i, (kt, var) in enumerate(ch):
                    P.add("pe", lambda e, i=i, kt=kt, hh=hh, t=t, pb=pb: e.matmul(
                        pb[:, i * 128:(i + 1) * 128], lhsT=kT[:, kt * 128:(kt + 1) * 128],
                        rhs=QAB[hh][:, t * 128:(t + 1) * 128], start=True, stop=False),
                        reads=[("kT", kt // 4), ("qT", t // 4)], writes=[pk])
                    P.add("pe", lambda e, i=i, var=var, hh=hh, pb=pb: e.matmul(
                        pb[:, i * 128:(i + 1) * 128], lhsT=identb[:], rhs=nabt[:, hh, var, :],
                        start=False, stop=True), reads=["identb", "nabt"], writes=[pk])
                E, ek = es[ci]
                w = 128 * len(ch)
                P.add("act", lambda e, E=E, pb=pb, w=w: e.activation(out=E[:, 0:w], in_=pb[:, 0:w], func=AF.Exp),
                      writes=[ek, pk])

        def stage_b(u, hp=hp):
            t, hh, ab, abk, chunks, es = u
            first_pv = True
            for ci, ch in enumerate(chunks):
                E, ek = es[ci]
                for i, (kt, var) in enumerate(ch):
                    last = (ci == len(chunks) - 1 and i == len(ch) - 1)
                    P.add("pe", lambda e, i=i, kt=kt, hh=hh, E=E, ab=ab, fp=first_pv, last=last: e.matmul(
                        ab[:, hh * 65:(hh + 1) * 65], lhsT=E[:, i * 128:(i + 1) * 128], rhs=vaugn[:, kt, hh, :],
                        start=(fp and hh == 0), stop=last, skip_group_check=True),
                        reads=[ek, ("v", kt // 4), "vones"], writes=[abk])
                    first_pv = False
            if hh == 1:
                rc = stat[:, 44:46]
                P.add("dve", lambda e, ab=ab: e.reciprocal(
                    out=rc, in_=ab[:, 0:130].rearrange("p (h c) -> p h c", h=2)[:, :, 64]),
                    writes=["rc2", abk])
                P.add("dve", lambda e, ab=ab, t=t: e.tensor_tensor(
                    out=otokb[t % 4].rearrange("p (h c) -> p h c", h=2),
                    in0=ab[:, 0:130].rearrange("p (h c) -> p h c", h=2)[:, :, 0:64],
                    in1=rc.unsqueeze(2).to_broadcast([128, 2, 64]), op=ALU.mult),
                    reads=["rc2"], writes=[("otokb", t % 4), abk])
                if t % 4 == 3:
                    transposes_to_oT(hp, t // 4, otokb, [("otokb", j) for j in range(4)])

        pipeline(units, stage_a, stage_b, 1)
    df_pre = {}

    def pre_df():
        df_pre[0] = load_win([O_DQ, O_DK, O_DV], force=0)
        P.dma("sp", lambda e: e.dma_start(out=alib, in_=alibi_d[0]), writes=["alib", "nabt"], stream="alibi")
    merge(1, wbr_d[1], False, after_first=pre_df)
    vt = vaug[:, :, 0:128]
    tA = rv(SB + 28800, 2048, F32)
    tB = rv(SB + 50816, 2048, F32)
    onesf = rv(SB + 52864, 512, F32)
    onesb = rv(SB + 53376, 256, BF16)
    P.add("dve", lambda e: e.memset(qT[64:128, :], 0.0), writes=[("qT", n) for n in range(4)])
    P.add("dve", lambda e: e.memset(QB[0:64, :], 0.0), writes=[("qT", n) for n in range(4)])
    P.add("dve", lambda e: e.memset(onesf, 1.0), writes=["onesf"])
    P.add("dve", lambda e: e.memset(onesb, 1.0), writes=["onesb"])
    sublnc = stat[:, 37:38]
    P.add("dve", lambda e: e.tensor_scalar(out=sublnc, in0=par[:, 440:441], scalar1=1.0 - LAM_INIT, scalar2=None,
                                           op0=ALU.mult), reads=["par"], writes=["sublnc"])
    poolD = Rot([bk(4), bk(5), bk(6)])
    spr3 = [spr[0], spr[1], rv(53248, 2048, F32)]
    for h in range(4):
        if h in df_pre:
            s = df_pre[h]
        else:
            s = load_win([O_DQ + h * 128, O_DK + h * 128, O_DV + h * 128], force=0)
            P.dma("sp", lambda e, h=h: e.dma_start(out=alib, in_=alibi_d[h]), writes=["alib"], stream="alibi")
        proj_fm(qT, 0, wsl[s][0], ("wsl", s, 0), 0.125, split=True)
        proj_fm(kT, 0, wsl[s][1], ("wsl", s, 1), 1.0)
        for g in range(4):
            pb, pk = poolC.next()
            for tt_ in range(4):
                t = 4 * g + tt_
                for k in range(KD):
                    P.add("pe", lambda e, k=k, t=t, tt_=tt_, pb=pb, s=s: e.matmul(
                        pb[:, tt_ * 128:(tt_ + 1) * 128], lhsT=hT[:, k, t * 128:(t + 1) * 128],
                        rhs=wsl[s][2][:, k, :], start=(k == 0), stop=(k == KD - 1)),
                        reads=[("wsl", s, 2), ("hT", t)], writes=[pk])
            P.add("dve", lambda e, g=g, pb=pb: e.tensor_copy(
                out=vt[:, 4 * g:4 * g + 4, :], in_=pb.rearrange("p (t c) -> p t c", t=4)),
                writes=[("v", g), pk])
        units = [(n, m, kt) for n in range(4) for m in range(2) for kt in range(NT)]
        OS = {0: (bk(0), bk(1)), 1: (bk(2), bk(3))}

        def stage_a(u, ui):
            n, m, kt = u
            pb, pk = poolD.next()
            P.add("pe", lambda e, m=m, kt=kt, n=n, pb=pb: e.matmul(
                pb, lhsT=kT[:, kt * 128:(kt + 1) * 128],
                rhs=QAB[m][:, n * 512:(n + 1) * 512], start=True, stop=True),
                reads=[("kT", kt // 4), ("qT", n)], writes=[pk])
            sp_, spk = spr3[ui % 3], ("spr", ui % 3)
            E, ek = Et[ui % 4], ("E", ui % 4)
            off = n * 512 - kt * 128 + 1920
            P.add("dve", lambda e, sp_=sp_, pb=pb, off=off: e.tensor_tensor(
                out=sp_, in0=pb, in1=alib[:, off:off + 512], op=ALU.add),
                reads=["alib"], writes=[spk, pk])
            P.add("act", lambda e, sp_=sp_, E=E: e.activation(out=E, in_=sp_, func=AF.Exp),
                  reads=[spk], writes=[ek])

        def stage_b(u, ui, h=h):
            n, m, kt = u
            E, ek = Et[ui % 4], ("E", ui % 4)
            (ob, obk), (sb, sbk) = OS[m]
            P.add("pe", lambda e, E=E, kt=kt, ob=ob: e.matmul(
                ob, lhsT=vt[:, kt, :], rhs=E, start=(kt == 0), stop=(kt == NT - 1)),
                reads=[ek, ("v", kt // 4)], writes=[obk])
            P.add("pe", lambda e, E=E, kt=kt, sb=sb: e.matmul(
                sb, lhsT=onesb, rhs=E, start=(kt == 0), stop=(kt == NT - 1)),
                reads=[ek, "onesb"], writes=[sbk])
            if m == 1 and kt == NT - 1:
                (o1, o1k), (s1, s1k) = OS[0]
                (o2, o2k), (s2, s2k) = OS[1]
                P.add("act", lambda e: e.activation(out=tA, in_=s1, func=AF.Ln), writes=["tA", s1k])
                P.add("act", lambda e: e.activation(out=tB, in_=s2, func=AF.Ln), writes=["tB", s2k])
                P.add("act", lambda e: e.activation(out=tA, in_=tA, func=AF.Exp, scale=-1.0),
                      reads=["tA"], writes=["tA"])
                P.add("act", lambda e: e.activation(out=tB, in_=tB, func=AF.Exp, scale=-1.0),
                      reads=["tB"], writes=["tB"])
                P.add("dve", lambda e: e.tensor_tensor(out=tA, in0=o1, in1=tA, op=ALU.mult),
                      reads=["tA"], writes=["tA", o1k])
                P.add("dve", lambda e: e.tensor_tensor(out=tB, in0=o2, in1=tB, op=ALU.mult),
                      reads=["tB"], writes=["tB", o2k])
                P.add("dve", lambda e: e.scalar_tensor_tensor(out=tA, in0=tB, scalar=nlam, in1=tA,
                                                              op0=ALU.mult, op1=ALU.add),
                      reads=["tB", "nlam", "tA"], writes=["tA"])
                P.add("act", lambda e: e.activation(out=tB, in_=tA, func=AF.Square), reads=["tA"], writes=["tB"])
                qb, qk = bk(7)
                P.add("pe", lambda e, qb=qb: e.matmul(qb, lhsT=onesf, rhs=tB, start=True, stop=True),
                      reads=["onesf", "tB"], writes=[qk])
                P.add("act", lambda e, qb=qb: e.activation(out=tB, in_=qb, func=AF.Ln, scale=1.0 / 128,
                                                            bias=epsb[:, 0:1]),
                      reads=["epsb"], writes=["tB", qk])
                P.add("act", lambda e: e.activation(out=tB, in_=tB, func=AF.Exp, scale=-0.5),
                      reads=["tB"], writes=["tB"])
                P.add("dve", lambda e, n=n, h=h: e.scalar_tensor_tensor(
                    out=oT[:, h, n * 512:(n + 1) * 512], in0=tA, scalar=sublnc, in1=tB,
                    op0=ALU.mult, op1=ALU.mult),
                    reads=["tA", "tB", "sublnc"], writes=[("oT", h, n)])

        pipeline(units, stage_a, stage_b, 2, with_index=True)
    wout = rv(SB, 16384, BF16, "p (k n) -> p k n", k=KD)
    wout_alias = ([("qT", n) for n in range(4)] + [("kT", n) for n in range(4)] + [("v", g) for g in range(4)]
                  + ["vones", ("wsl", 0, 0), ("wsl", 0, 1)])
    def pre_wout():
        P.dma("pool", lambda e: e.dma_start(out=wout, in_=wout_d[:, :].rearrange("(k p) n -> p k n", p=128)),
              writes=["wout"] + wout_alias, stream="wout")
    merge(2, wbr_d[2], False, after_first=pre_wout)

    oT_keys = [("oT", k, n) for k in range(4) for n in range(4)]
    wgs2 = [rv(24576 + i * 8192, 4096, BF16, "p (k n) -> p k n", k=KD) for i in range(3)]
    wus2 = [rv(24576 + i * 8192 + 4096, 4096, BF16, "p (k n) -> p k n", k=KD) for i in range(3)]
    for pi, sl in ((0, 1), (1, 2)):
        P.dma("pool", lambda e, pi=pi, sl=sl: e.dma_start(
            out=wgs2[sl], in_=w2g[:, pi * 256:(pi + 1) * 256].rearrange("(k p) n -> p k n", p=128)),
            writes=[("wg", sl)] + oT_keys, stream="wgu%d" % sl)
        P.dma("pool", lambda e, pi=pi, sl=sl: e.dma_start(
            out=wus2[sl], in_=w2u[:, pi * 256:(pi + 1) * 256].rearrange("(k p) n -> p k n", p=128)),
            writes=[("wu", sl)] + oT_keys, stream="wgu%d" % sl)
    pop2 = Rot([bk(4), bk(5), bk(6), bk(7)])
    for t in range(NT):
        for j in range(2):
            po, pok = pop2.next()
            for k in range(KD):
                P.add("pe", lambda e, k=k, po=po, t=t, j=j: e.matmul(
                    po, lhsT=mergedT[:, k, t * 128:(t + 1) * 128], rhs=wout[:, k, j * 512:(j + 1) * 512],
                    start=(k == 0), stop=(k == KD - 1)),
                    reads=["wout", ("mT", k, t // 4)], writes=[pok])
            P.add("dve", lambda e, po=po, t=t, j=j: e.tensor_tensor(
                out=xs[:, t, j * 512:(j + 1) * 512], in0=po, in1=xs[:, t, j * 512:(j + 1) * 512], op=ALU.add),
                writes=[("xs", t), pok])

    for g in range(4):
        norm_stats(g, junk_f)
    trp2 = Rot([bkb(6), bkb(7)])
    ffn(w2g, w2u, w2d, "f2", pidx0=1, preloaded=2, pre_unit=lambda n: norm_emit(n, 16, hb_f, trp2))

    gfin = rv(83968, 4096, F32)
    outb = [rv(88064 + i * 4096, 4096, F32) for i in range(4)]
    P.dma("sp", lambda e: e.dma_start(out=gfin, in_=gfin_d[:, :]), reads=[("xs", 0)], writes=["gfin"], stream="gfin")
    junk_o = rv(104448, 2048, BF16)
    for g in range(4):
        for t in range(4 * g, 4 * g + 4):
            P.add("act", lambda e, t=t: e.activation(out=junk_o, in_=xs[:, t, :], func=AF.Square,
                                                      accum_out=ss[:, t:t + 1]),
                  reads=[("xs", t)], writes=["junk", ("ss", t)])
        rstd_ops(ss[:, 4 * g:4 * g + 4], rstd[:, 4 * g:4 * g + 4], D, [("ss", u) for u in range(4 * g, 4 * g + 4)],
                 ("rstd", g), 4)
        for t in range(4 * g, 4 * g + 4):
            ob = outb[t % 4]
            P.add("dve", lambda e, t=t, ob=ob: e.scalar_tensor_tensor(
                out=ob, in0=xs[:, t, :], scalar=rstd[:, t:t + 1], in1=gfin, op0=ALU.mult, op1=ALU.mult),
                reads=[("xs", t), ("rstd", g), "gfin"], writes=[("ob", t % 4)])
            P.dma("sp", lambda e, t=t, ob=ob: e.dma_start(out=y_d[t * 128:(t + 1) * 128, :], in_=ob),
                  reads=[("ob", t % 4)], stream="out%d" % (t % 4))
    counts = P.emit(final_wait_streams=["out0", "out1", "out2", "out3"])
    return nc, counts


_CACHE = {}


def kernel(x, mem, ffn1_norm, ffn1_w_gate, ffn1_w_up, ffn1_w_down, mix_norm, w_in, na_rpb,
           diff_lambda_q1, diff_lambda_k1, diff_lambda_q2, diff_lambda_k2, diff_subln,
           mem_norm, w_mem_kv, w_gate, b_gate, w_br_na, w_br_diff, w_br_mem, w_out,
           ffn2_norm, ffn2_w_gate, ffn2_w_up, ffn2_w_down, final_norm):
    f = lambda a: np.ascontiguousarray(np.asarray(a, dtype=np.float32))
    x = f(x)
    mem = f(mem)
    colT = lambda v: f(v).reshape(-1, 128).T
    rep = lambda v: np.broadcast_to(f(v).reshape(1, -1), (128, f(v).size))
    par = np.concatenate([
        colT(ffn1_norm[0]), colT(mix_norm[0]), colT(ffn2_norm[0]), colT(mem_norm[0]), colT(b_gate[0]),
        rep(diff_lambda_q1[0]), rep(diff_lambda_k1[0]), rep(diff_lambda_q2[0]), rep(diff_lambda_k2[0]),
        rep(diff_subln[0]), colT(diff_subln[0])], axis=1)
    par = np.ascontiguousarray(par, dtype=np.float32)
    assert par.shape == (128, 441)
    gfin = np.ascontiguousarray(rep(final_norm))
    shared = {
        "ffn1_w_gate": f(ffn1_w_gate)[0], "ffn1_w_up": f(ffn1_w_up)[0], "ffn1_w_down": f(ffn1_w_down)[0],
        "ffn2_w_gate": f(ffn2_w_gate)[0], "ffn2_w_up": f(ffn2_w_up)[0], "ffn2_w_down": f(ffn2_w_down)[0],
        "w_in": f(w_in)[0], "w_mem_kv": f(w_mem_kv)[0], "w_gate": f(w_gate)[0],
        "w_br_mem": f(w_br_mem)[0], "w_br_na": f(w_br_na)[0], "w_br_diff": f(w_br_diff)[0],
        "w_out": f(w_out)[0], "params": par, "gfin": gfin,
        "ident": np.eye(128, dtype=np.float32), "alibi": build_alibi(),
        "nab": np.ascontiguousarray(build_nab(f(na_rpb)[0]).reshape(4, 2, 21, 128, 128).transpose(0, 3, 1, 2, 4)
                                    .reshape(4, 128, 2 * 21 * 128)),
    }
    if "nc" not in _CACHE:
        _CACHE["nc"] = build_nc()[0]
    nc = _CACHE["nc"]
    in_maps = []
    for b in range(8):
        m = dict(shared)
        m["x"] = x[b]
        m["mem"] = mem[b]
        in_maps.append(m)
    res = run_bass_kernel_spmd(nc, in_maps, core_ids=list(range(8)))
    return np.stack([np.asarray(r["y"], dtype=np.float32) for r in res.results], axis=0)
```

```python
import contextlib
import numpy as np
import concourse.bass as bass
import concourse.mybir as mybir
from concourse.bass_utils import run_bass_kernel_spmd

F32 = mybir.dt.float32
BF16 = mybir.dt.bfloat16
AF = mybir.ActivationFunctionType
ALU = mybir.AluOpType

S = 2048
D = 1024
NT = 16
KD = 8
DFF = 2816
NCH = 22
MEMT = 256
EPS = 1e-6
LAM_INIT = 0.2
SLOPES = [2.0 ** (-8.0 * (i + 1) / 4) for i in range(4)]
O_NQ, O_NK, O_NV, O_DQ, O_DK, O_DV, O_MQ = 0, 512, 1024, 1536, 2048, 2560, 3072
ALW = 3968
MASKV = -30000.0
SAME_ENGINE_SYNC = True
ATTACH_WAIT = True

ENGINES = ("pe", "act", "dve", "pool", "sp")


class Op:
    __slots__ = ("idx", "eng", "fn", "reads", "writes", "is_dma", "stream", "deps",
                 "signal", "count", "deps_x")

    def __init__(self, idx, eng, fn, reads, writes, is_dma, stream):
        self.idx = idx
        self.eng = eng
        self.fn = fn
        self.reads = reads
        self.writes = writes
        self.is_dma = is_dma
        self.stream = stream
        self.deps = set()
        self.deps_x = set()
        self.signal = False
        self.count = 0


class Prog:
    def __init__(self, nc, same_engine_sync=True):
        self.nc = nc
        self.ops = []
        self.last_writer = {}
        self.readers = {}
        self.same_engine_sync = same_engine_sync
        self.stream_names = []
        self.last_of_eng = {}
        self.last_of_stream = {}
        self.pending = {}

    def fence(self):
        snap = set(self.last_of_eng.values()) | set(self.last_of_stream.values())
        for e in ENGINES:
            self.pending[e] = set(snap) | self.pending.get(e, set())

    def _add(self, eng, fn, reads, writes, is_dma, stream):
        op = Op(len(self.ops), eng, fn, tuple(reads), tuple(writes), is_dma, stream)
        deps = set()
        deps_x = set()
        for r in op.reads:
            lw = self.last_writer.get(r)
            if lw is not None:
                deps.add(lw)
        for w in op.writes:
            tgt = deps_x if (isinstance(w, tuple) and w[0] == "PS") else deps
            lw = self.last_writer.get(w)
            if lw is not None:
                tgt.add(lw)
            for rd in self.readers.get(w, ()):
                tgt.add(rd)
        pend = self.pending.pop(eng, None)
        if pend:
            deps_x |= pend
        deps.discard(op.idx)
        deps_x.discard(op.idx)
        op.deps = deps
        op.deps_x = deps_x - deps
        for r in op.reads:
            self.readers.setdefault(r, []).append(op.idx)
        for w in op.writes:
            self.last_writer[w] = op.idx
            self.readers[w] = []
        self.ops.append(op)
        if is_dma:
            self.last_of_stream[stream] = op.idx
        else:
            self.last_of_eng[eng] = op.idx
        return op

    def add(self, eng, fn, reads=(), writes=()):
        return self._add(eng, fn, reads, writes, False, None)

    def dma(self, eng, fn, reads=(), writes=(), stream=None):
        if stream not in self.stream_names:
            self.stream_names.append(stream)
        return self._add(eng, fn, reads, writes, True, stream)

    def emit(self, final_wait_streams=()):
        nc = self.nc
        ops = self.ops

        def skip_dep(op, dop, is_x):
            if dop.is_dma or op.is_dma:
                return False
            if dop.eng != op.eng:
                return False
            if is_x or dop.eng == "pe" or not self.same_engine_sync:
                return True
            return False

        for op in ops:
            latest = {}
            for dset, is_x in ((op.deps, False), (op.deps_x, True)):
                for d in dset:
                    dop = ops[d]
                    if not dop.is_dma and not skip_dep(op, dop, is_x):
                        if latest.get(dop.eng, -1) < d:
                            latest[dop.eng] = d
            op.deps = set(d for d in op.deps if ops[d].is_dma)
            op.deps_x = set(d for d in op.deps_x if ops[d].is_dma)
            for d in latest.values():
                ops[d].signal = True
                op.deps_x.add(d)
        self_skip = skip_dep

        def skip_dep(op, dop, is_x):
            return False
        eng_count = {e: 0 for e in ENGINES}
        stream_count = {s: 0 for s in self.stream_names}
        stream_hist = {s: [] for s in self.stream_names}
        for op in ops:
            if op.is_dma:
                stream_count[op.stream] += 16
                op.count = stream_count[op.stream]
                stream_hist[op.stream].append((op.idx, op.count))
            elif op.signal:
                eng_count[op.eng] += 1
                op.count = eng_count[op.eng]
        with contextlib.ExitStack() as es:
            eng_sem = {e: es.enter_context(nc.semaphore("s_" + e)) for e in ENGINES}
            st_sem = {s: es.enter_context(nc.semaphore("d_" + str(i)))
                      for i, s in enumerate(self.stream_names)}
            block = es.enter_context(nc.Block())
            per_eng = {e: [op for op in ops if op.eng == e] for e in ENGINES}

            def stream_value_before(stream, idx):
                v = 0
                for (i, c) in stream_hist[stream]:
                    if i < idx:
                        v = c
                    else:
                        break
                return v

            vc = [None] * len(ops)
            know = {e: {} for e in ENGINES}
            last_sig = {e: 0 for e in ENGINES}
            waits_of = [None] * len(ops)

            def merge_into(dst, src):
                for k, v in src.items():
                    if dst.get(k, 0) < v:
                        dst[k] = v

            for op in ops:
                K = know[op.eng]
                need = {}
                for d in list(op.deps) + list(op.deps_x):
                    dop = ops[d]
                    if dop.is_dma:
                        key = ("d", dop.stream)
                        val = stream_value_before(dop.stream, op.idx)
                    else:
                        key = ("e", dop.eng)
                        val = dop.count
                    if need.get(key, (0, None))[0] < val:
                        need[key] = (val, d)
                wl = []
                for key, (val, d) in sorted(need.items(), key=lambda kv: -kv[1][1]):
                    if K.get(key, 0) >= val:
                        continue
                    wl.append((key, val))
                    if K.get(key, 0) < val:
                        K[key] = val
                    merge_into(K, vc[d])
                waits_of[op.idx] = wl
                v = dict(K)
                if op.is_dma:
                    v[("d", op.stream)] = max(v.get(("d", op.stream), 0), op.count)
                else:
                    if op.signal:
                        last_sig[op.eng] = op.count
                    v[("e", op.eng)] = max(v.get(("e", op.eng), 0), last_sig[op.eng])
                    if not self.same_engine_sync or op.eng == "pe":
                        pass
                vc[op.idx] = v
            self.n_waits = sum(len(w) for w in waits_of)

            def run(engname, eng):
                for op in per_eng[engname]:
                    wl = waits_of[op.idx]
                    attach = None
                    if ATTACH_WAIT and wl and not op.is_dma:
                        attach = wl[-1]
                        wl = wl[:-1]
                    for key, val in wl:
                        sem = st_sem[key[1]] if key[0] == "d" else eng_sem[key[1]]
                        eng.wait_ge(sem, val)
                    ins = op.fn(eng)
                    if attach is not None:
                        key, val = attach
                        sem = st_sem[key[1]] if key[0] == "d" else eng_sem[key[1]]
                        ins._wait_ge(sem, val)
                    if op.is_dma:
                        ins.then_inc(st_sem[op.stream], 16)
                    elif op.signal:
                        ins.then_inc(eng_sem[op.eng], 1)
                if engname == "sp":
                    for s in final_wait_streams:
                        eng.wait_ge(st_sem[s], stream_count[s])

            block.tensor(lambda e: run("pe", e))
            block.scalar(lambda e: run("act", e))
            block.vector(lambda e: run("dve", e))
            block.gpsimd(lambda e: run("pool", e))
            block.sync(lambda e: run("sp", e))
        return eng_count, stream_count


class Rot:
    def __init__(self, items):
        self.items = items
        self.i = 0

    def next(self):
        it = self.items[self.i % len(self.items)]
        self.i += 1
        return it


def na_key_tiles(t):
    if t <= 1:
        return [(kt, 5 + 4 * t + kt) for kt in range(4)]
    if t >= 14:
        return [(kt, 13 + 4 * (t - 14) + (kt - 12)) for kt in range(12, 16)]
    return [(kt, kt - t + 2) for kt in range(t - 2, t + 3)]


def na_variant_reps():
    reps = {}
    for t in (5, 0, 1, 14, 15):
        for kt, v in na_key_tiles(t):
            reps[v] = (t, kt)
    return [reps[v] for v in range(21)]


def build_nab(rpb):
    out = np.empty((8, 21, 128, 128), np.float32)
    idx = np.arange(128)
    for v, (t, kt) in enumerate(na_variant_reps()):
        qr = 2 * t + idx // 64
        qc = idx % 64
        kr = 2 * kt + idx // 64
        kc = idx % 64
        rs = np.clip(qr - 4, 0, 24)
        cs = np.clip(qc - 8, 0, 48)
        KR, QR = kr[:, None], qr[None, :]
        KC, QC = kc[:, None], qc[None, :]
        inside = (KR >= rs[None, :]) & (KR < rs[None, :] + 8) & (KC >= cs[None, :]) & (KC < cs[None, :] + 16)
        dr = np.clip(KR - QR + 7, 0, 14)
        dc = np.clip(KC - QC + 15, 0, 30)
        g = rpb[:, dr, dc]
        out[:, v] = np.where(inside[None], g, np.float32(MASKV))
    return out


def build_alibi():
    k = np.arange(128, dtype=np.float64)[:, None]
    u = np.arange(ALW, dtype=np.float64)[None, :] - 1920.0
    dist = np.abs(u - k)
    return np.stack([(-s * dist).astype(np.float32) for s in SLOPES])


def build_nc():
    nc = bass.Bass("TRN2", target_bir_lowering=False)

    def din(name, shape):
        return nc.dram_tensor(name, list(shape), F32, kind="ExternalInput").ap()

    x_d = din("x", [S, D])
    mem_d = din("mem", [MEMT, D])
    w1g, w1u, w1d = din("ffn1_w_gate", [D, DFF]), din("ffn1_w_up", [D, DFF]), din("ffn1_w_down", [DFF, D])
    w2g, w2u, w2d = din("ffn2_w_gate", [D, DFF]), din("ffn2_w_up", [D, DFF]), din("ffn2_w_down", [DFF, D])
    win_d = din("w_in", [D, 3584])
    wkv_d = din("w_mem_kv", [D, 1024])
    wgate_d = din("w_gate", [D, 3072])
    wbr_d = [din("w_br_mem", [512, D]), din("w_br_na", [512, D]), din("w_br_diff", [512, D])]
    wout_d = din("w_out", [D, D])
    NPAR = 441
    par_d = din("params", [128, NPAR])
    gfin_d = din("gfin", [128, D])
    ident_d = din("ident", [128, 128])
    alibi_d = din("alibi", [4, 128, ALW])
    nab_d = din("nab", [4, 128, 2 * 21 * 128])
    y_d = nc.dram_tensor("y", [S, D], F32, kind="ExternalOutput").ap()

    xs = nc.alloc_sbuf_tensor("xs", [128, NT, D], F32)
    hT = nc.alloc_sbuf_tensor("hT", [128, KD, S], BF16)
    par = nc.alloc_sbuf_tensor("par", [128, NPAR], F32)
    identb = nc.alloc_sbuf_tensor("identb", [128, 128], BF16)
    stat = nc.alloc_sbuf_tensor("stat", [128, 64], F32)
    RB = 110592
    R = nc.alloc_sbuf_tensor("R", [128, RB // 4], F32)

    def rv(off, nbytes, dt, pat=None, **kw):
        assert off % 4 == 0 and nbytes % 4 == 0 and off + nbytes <= RB
        v = R[:, off // 4:(off + nbytes) // 4]
        if dt is not F32:
            v = v.bitcast(dt)
        if pat:
            v = v.rearrange(pat, **kw)
        return v

    banks = [nc.alloc_psum_tensor("B%d" % i, [128, 512], F32) for i in range(8)]

    def bk(i):
        return banks[i][:, :], ("PS", i)

    def bkb(i):
        return banks[i][:, :].bitcast(BF16), ("PS", i)

    P = Prog(nc, same_engine_sync=SAME_ENGINE_SYNC)

    for i in range(8):
        P.dma("sp" if i % 2 == 0 else "act", lambda e, i=i: e.dma_start(
            out=xs[:, 2 * i:2 * i + 2, :],
            in_=x_d[256 * i:256 * (i + 1), :].rearrange("(t p) d -> p t d", p=128)),
            writes=[("xs", t) for t in range(2 * i, 2 * i + 2)], stream="x%d" % i)
    P.dma("sp", lambda e: e.dma_start(out=par[:], in_=par_d[:, :]), writes=["par"], stream="par")
    P.dma("pool", lambda e: e.dma_start(out=identb[:], in_=ident_d[:, :]), writes=["identb"], stream="ident")

    def wload(dst, src2d, key, stream, pat="(k p) n -> p k n", alias=()):
        P.dma("pool", lambda e: e.dma_start(out=dst, in_=src2d.rearrange(pat, p=128)),
              writes=[key] + list(alias), stream=stream)

    rstd_ctr = [0]

    def rstd_ops(ss_ap, out_ap, n, rkeys, wkey, width):
        assert width <= 4
        c0 = 48 + (rstd_ctr[0] % 4) * 4
        rstd_ctr[0] += 1
        tmp = stat[:, c0:c0 + width]
        tk = ("stat_tmp", c0)
        P.add("act", lambda e: e.activation(out=tmp, in_=ss_ap, func=AF.Ln, scale=1.0 / n, bias=epsb[:, 0:1]),
              reads=list(rkeys) + ["epsb"], writes=[tk])
        P.add("act", lambda e: e.activation(out=out_ap, in_=tmp, func=AF.Exp, scale=-0.5),
              reads=[tk], writes=[wkey])

    epsb = nc.alloc_sbuf_tensor("epsb", [128, 1], F32)
    P.add("dve", lambda e: e.memset(epsb[:], EPS), writes=["epsb"])

    ss = stat[:, 0:16]
    rstd = stat[:, 16:32]

    def norm_to_hT(gcol, hb, junk, trpool):
        for t in range(NT):
            P.add("act", lambda e, t=t: e.activation(out=junk, in_=xs[:, t, :], func=AF.Square,
                                                      accum_out=ss[:, t:t + 1]),
                  reads=[("xs", t)], writes=["junk", ("ss", t)])
            if t % 4 == 3:
                g0 = t - 3
                rstd_ops(ss[:, g0:g0 + 4], rstd[:, g0:g0 + 4], D, [("ss", u) for u in range(g0, g0 + 4)],
                         ("rstd", g0 // 4), 4)
        for t in range(NT):
            hbt = hb[t % 2]
            hk = ("hb", t % 2)
            P.add("act", lambda e, t=t, hbt=hbt: e.activation(
                out=hbt, in_=xs[:, t, :], func=AF.Copy, scale=rstd[:, t:t + 1]),
                reads=[("xs", t), ("rstd", t // 4)], writes=[hk])
            pb, pk = trpool.next()
            for k in range(KD):
                P.add("pe", lambda e, k=k, hbt=hbt, pb=pb: e.transpose(
                    out=pb[:, k * 128:(k + 1) * 128], in_=hbt[:, k * 128:(k + 1) * 128], identity=identb[:]),
                    reads=[hk, "identb"], writes=[pk])
            P.add("dve", lambda e, t=t, pb=pb: e.tensor_tensor(
                out=hT[:, :, t * 128:(t + 1) * 128],
                in0=pb[:, 0:1024].rearrange("p (k n) -> p k n", k=KD),
                in1=par[:, gcol:gcol + KD].unsqueeze(2).to_broadcast([128, KD, 128]),
                op=ALU.mult),
                reads=["par"], writes=[("hT", t), pk])

    WG_ALIAS = {0: [("mT", c, n) for c in (6, 7) for n in range(4)],
                1: [("oT", k, n) for k in (0, 1) for n in range(4)],
                2: [("oT", k, n) for k in (2, 3) for n in range(4)]}
    WD_ALIAS = {0: ["wout", ("spr", 2)] + [("qT", n) for n in range(4)] + [("qB", n) for n in range(4)],
                1: ["wout", ("wsl", 0, 0), ("wsl", 0, 1), ("wsl", 0, 2)]}

    def ffn(wg_d, wu_d, wd_d, tag, hook_a=None, hook_b=None, hook_c=None, pidx0=0, preloaded=0):
        actT = rv(0, 24576, BF16, "p (c n) -> p c n", c=6)
        wgs = [rv(24576 + i * 8192, 4096, BF16, "p (k n) -> p k n", k=KD) for i in range(3)]
        wus = [rv(24576 + i * 8192 + 4096, 4096, BF16, "p (k n) -> p k n", k=KD) for i in range(3)]
        wds = [rv(49152 + i * 12288, 12288, BF16, "p (c n) -> p c n", c=6) for i in range(2)]
        sgs = [rv(77824 + i * 2048, 2048, F32) for i in range(2)]
        pgp = Rot([bk(0), bk(1)])
        pup = Rot([bk(2), bk(3)])
        pop = Rot([bk(4), bk(5)])
        groups = [[0, 1, 2], [3, 4, 5], [6, 7, 8], [9, 10]]
        pidx = pidx0
        npl = 0
        ev = 0
        for gi, grp in enumerate(groups):
            ncg = 2 * len(grp)
            c0 = 2 * grp[0]
            wd = wds[gi % 2]
            for pi in grp:
                sl = pidx % 3
                pidx += 1
                if npl >= preloaded:
                    wload(wgs[sl], wg_d[:, pi * 256:(pi + 1) * 256], ("wg", sl), "wgu%d" % sl, alias=WG_ALIAS[sl])
                    wload(wus[sl], wu_d[:, pi * 256:(pi + 1) * 256], ("wu", sl), "wgu%d" % sl, alias=WG_ALIAS[sl])
                npl += 1
                if pi == grp[0]:
                    wload(wd[:, 0:ncg, :], wd_d[c0 * 128:(c0 + ncg) * 128, :], ("wd", gi % 2), "wd%d" % (gi % 2),
                          pat="(c p) n -> p c n", alias=WD_ALIAS[gi % 2])
                for cc in range(2):
                    cl = (pi - grp[0]) * 2 + cc
                    for n in range(4):
                        pg, pgk = pgp.next()
                        pu, puk = pup.next()
                        hkeys = [("hT", t) for t in range(4 * n, 4 * n + 4)]
                        for k in range(KD):
                            P.add("pe", lambda e, k=k, pg=pg, sl=sl, cc=cc, n=n: e.matmul(
                                pg, lhsT=wgs[sl][:, k, cc * 128:(cc + 1) * 128],
                                rhs=hT[:, k, n * 512:(n + 1) * 512], start=(k == 0), stop=(k == KD - 1)),
                                reads=[("wg", sl)] + hkeys, writes=[pgk])
                        for k in range(KD):
                            P.add("pe", lambda e, k=k, pu=pu, sl=sl, cc=cc, n=n: e.matmul(
                                pu, lhsT=wus[sl][:, k, cc * 128:(cc + 1) * 128],
                                rhs=hT[:, k, n * 512:(n + 1) * 512], start=(k == 0), stop=(k == KD - 1)),
                                reads=[("wu", sl)] + hkeys, writes=[puk])
                        sg = sgs[ev % 2]
                        sgk = ("sg", ev % 2)
                        ev += 1
                        P.add("act", lambda e, sg=sg, pg=pg: e.activation(out=sg, in_=pg, func=AF.Silu),
                              writes=[sgk, pgk])
                        P.add("dve", lambda e, sg=sg, pu=pu, cl=cl, n=n: e.tensor_tensor(
                            out=actT[:, cl, n * 512:(n + 1) * 512], in0=pu, in1=sg, op=ALU.mult),
                            reads=[sgk], writes=[("actT", cl, n), puk])
                if hook_a is not None and gi == 0 and pi == grp[0]:
                    hook_a()
            if hook_b is not None and gi == 0:
                hook_b()
            for t in range(NT):
                for j in range(2):
                    po, pok = pop.next()
                    for c in range(ncg):
                        P.add("pe", lambda e, c=c, po=po, t=t, j=j, wd=wd, ncg=ncg: e.matmul(
                            po, lhsT=actT[:, c, t * 128:(t + 1) * 128], rhs=wd[:, c, j * 512:(j + 1) * 512],
                            start=(c == 0), stop=(c == ncg - 1)),
                            reads=[("actT", c, t // 4), ("wd", gi % 2)], writes=[pok])
                    P.add("dve", lambda e, po=po, t=t, j=j: e.scalar_tensor_tensor(
                        out=xs[:, t, j * 512:(j + 1) * 512], in0=po, scalar=0.5,
                        in1=xs[:, t, j * 512:(j + 1) * 512], op0=ALU.mult, op1=ALU.add),
                        writes=[("xs", t), pok])
            if hook_c is not None and gi == 0:
                hook_c()


    memraw = rv(83968, 8192, F32, "p (t d) -> p t d", t=2)
    memnb = rv(92160, 4096, BF16, "p (t d) -> p t d", t=2)
    memnT = rv(96256, 4096, BF16, "p (k n) -> p k n", k=KD)
    wkc = [rv(100352, 4096, BF16, "p (k n) -> p k n", k=KD),
           rv(83968, 4096, BF16, "p (k n) -> p k n", k=KD),
           rv(88064, 4096, BF16, "p (k n) -> p k n", k=KD),
           rv(92160, 4096, BF16, "p (k n) -> p k n", k=KD)]
    mv = rv(104448, 2112, BF16, "p (t h c) -> p t h c", t=2, h=4)
    mkT = rv(106560, 2048, BF16, "p (h n) -> p h n", h=4)
    ssm = stat[:, 38:40]
    rsm = stat[:, 40:42]

    def mem_prep_a():
        P.dma("sp", lambda e: e.dma_start(out=memraw, in_=mem_d.rearrange("(t p) d -> p t d", p=128)),
              writes=["memraw"], stream="mem")
        wload(wkc[0], wkv_d[:, 0:256], ("wkc", 0), "wkc0")
        for t in range(2):
            P.add("act", lambda e, t=t: e.activation(out=junk_f, in_=memraw[:, t, :], func=AF.Square,
                                                      accum_out=ssm[:, t:t + 1]),
                  reads=["memraw"], writes=["junk", ("ssm", t)])
        rstd_ops(ssm, rsm, D, [("ssm", 0), ("ssm", 1)], "rsm", 2)
        for t in range(2):
            P.add("dve", lambda e, t=t: e.tensor_scalar(out=memnb[:, t, :], in0=memraw[:, t, :],
                                                        scalar1=rsm[:, t:t + 1], scalar2=None, op0=ALU.mult),
                  reads=["memraw", "rsm"], writes=[("memnb", t)])

    def mem_prep_b():
        P.dma("pool", lambda e: e.dma_start(out=wkc[1], in_=wkv_d[:, 256:512].rearrange("(k p) n -> p k n", p=128)),
              writes=[("wkc", 1), "memraw"], stream="wkc1")
        P.dma("pool", lambda e: e.dma_start(out=wkc[2], in_=wkv_d[:, 512:768].rearrange("(k p) n -> p k n", p=128)),
              writes=[("wkc", 2), "memraw"], stream="wkc2")
        for t in range(2):
            pb, pk = bkb(6 + t)
            for k in range(KD):
                P.add("pe", lambda e, k=k, t=t, pb=pb: e.transpose(
                    out=pb[:, k * 128:(k + 1) * 128], in_=memnb[:, t, k * 128:(k + 1) * 128], identity=identb[:]),
                    reads=[("memnb", t), "identb"], writes=[pk])
            P.add("dve", lambda e, t=t, pb=pb: e.tensor_tensor(
                out=memnT[:, :, t * 128:(t + 1) * 128],
                in0=pb[:, 0:1024].rearrange("p (k n) -> p k n", k=KD),
                in1=par[:, 24:32].unsqueeze(2).to_broadcast([128, KD, 128]), op=ALU.mult),
                reads=["par"], writes=["memnT", pk])
        P.dma("pool", lambda e: e.dma_start(out=wkc[3], in_=wkv_d[:, 768:1024].rearrange("(k p) n -> p k n", p=128)),
              writes=[("wkc", 3), ("memnb", 0), ("memnb", 1)], stream="wkc3")

    def mem_prep_c():
        P.add("dve", lambda e: e.memset(mv[:, :, :, 128:129], 1.0), writes=["mv"])
        mpool = Rot([bk(6), bk(7)])
        for ci in range(4):
            sl = ci
            if ci < 2:
                for hh in range(2):
                    h = 2 * ci + hh
                    pb, pk = mpool.next()
                    for k in range(KD):
                        P.add("pe", lambda e, k=k, hh=hh, pb=pb, sl=sl: e.matmul(
                            pb[:, 0:256], lhsT=wkc[sl][:, k, hh * 128:(hh + 1) * 128], rhs=memnT[:, k, :],
                            start=(k == 0), stop=(k == KD - 1)),
                            reads=[("wkc", sl), "memnT"], writes=[pk])
                    P.add("act", lambda e, h=h, pb=pb: e.activation(out=mkT[:, h, :], in_=pb[:, 0:256], func=AF.Copy),
                          writes=["mkT", pk])
            else:
                for t in range(2):
                    pb, pk = mpool.next()
                    for k in range(KD):
                        P.add("pe", lambda e, k=k, t=t, pb=pb, sl=sl: e.matmul(
                            pb[:, 0:256], lhsT=memnT[:, k, t * 128:(t + 1) * 128], rhs=wkc[sl][:, k, :],
                            start=(k == 0), stop=(k == KD - 1)),
                            reads=[("wkc", sl), "memnT"], writes=[pk])
                    h0 = 2 * (ci - 2)
                    P.add("dve", lambda e, t=t, pb=pb, h0=h0: e.tensor_copy(
                        out=mv[:, t, h0:h0 + 2, 0:128], in_=pb[:, 0:256].rearrange("p (h c) -> p h c", h=2)),
                        reads=["mv"], writes=["mv2", pk])

    hb_f = [rv(73728, 2048, BF16), rv(75776, 2048, BF16)]
    junk_f = rv(81920, 2048, BF16)
    trp = Rot([bkb(6), bkb(7)])

    norm_to_hT(0, hb_f, junk_f, trp)
    ffn(w1g, w1u, w1d, "f1", hook_a=mem_prep_a, hook_b=mem_prep_b, hook_c=mem_prep_c)
    norm_to_hT(8, hb_f, junk_f, Rot([bkb(6), bkb(7)]))
    P.fence()

    mergedT = rv(0, 32768, BF16, "p (k n) -> p k n", k=KD)
    oT = rv(32768, 16384, BF16, "p (k n) -> p k n", k=4)
    hb_m = [rv(49152, 2048, BF16), rv(51200, 2048, BF16)]
    junk_m = rv(53248, 2048, BF16)
    SB = 55296
    qT = rv(SB, 4096, BF16)
    kT = rv(SB + 4096, 4096, BF16)
    vaug = rv(SB + 8192, 4224, BF16, "p (t c) -> p t c", t=NT)
    wsl = [[rv(SB + 12416 + s * 6144 + i * 2048, 2048, BF16, "p (k n) -> p k n", k=KD) for i in range(3)]
           for s in range(2)]
    Et = [rv(SB + 24704 + i * 1024, 1024, BF16) for i in range(4)]
    otok = rv(SB + 28800, 1024, F32)
    otokb = [rv(SB + 29824 + i * 256, 256, BF16) for i in range(4)]
    spr = [rv(SB + 30848 + i * 2048, 2048, F32) for i in range(2)]
    XO = SB + 34944
    alib = rv(XO, 15872, F32)
    nabt = rv(XO, 10752, BF16, "p (h v q) -> p h v q", h=2, v=21)
    nabt_flat = rv(XO, 10752, BF16)

    poolA = Rot([bk(0), bk(1), bk(2)])
    poolB = Rot([bk(3), bk(4), bk(5)])
    poolC = Rot([bk(6), bk(7)])


    lamt = stat[:, 32:36]
    P.add("dve", lambda e: e.tensor_tensor(out=otok[:, 0:64], in0=par[:, 56:120], in1=par[:, 120:184], op=ALU.mult),
          reads=["par"], writes=["otok"])
    P.add("dve", lambda e: e.reduce_sum(out=lamt[:, 0:1], in_=otok[:, 0:64], axis=mybir.AxisListType.X),
          reads=["otok"], writes=["lam0"])
    P.add("dve", lambda e: e.tensor_tensor(out=otok[:, 64:128], in0=par[:, 184:248], in1=par[:, 248:312], op=ALU.mult),
          reads=["par"], writes=["otok2"])
    P.add("dve", lambda e: e.reduce_sum(out=lamt[:, 1:2], in_=otok[:, 64:128], axis=mybir.AxisListType.X),
          reads=["otok2"], writes=["lam1"])
    P.add("act", lambda e: e.activation(out=lamt[:, 2:4], in_=lamt[:, 0:2], func=AF.Exp),
          reads=["lam0", "lam1"], writes=["lam2"])
    nlam = stat[:, 36:37]
    P.add("dve", lambda e: e.scalar_tensor_tensor(out=nlam, in0=lamt[:, 3:4], scalar=-LAM_INIT, in1=lamt[:, 2:3],
                                                  op0=ALU.add, op1=ALU.subtract),
          reads=["lam2"], writes=["nlam"])
    QB = rv(49152, 4096, BF16)
    QAB = [qT, QB]

    def proj_fm(dst, col0, slot_ap, wkey, scale, split=False):
        for n in range(4):
            pb, pk = poolC.next()
            for k in range(KD):
                P.add("pe", lambda e, k=k, n=n, pb=pb: e.matmul(
                    pb, lhsT=slot_ap[:, k, :], rhs=hT[:, k, n * 512:(n + 1) * 512],
                    start=(k == 0), stop=(k == KD - 1)),
                    reads=[wkey] + [("hT", t) for t in range(4 * n, 4 * n + 4)], writes=[pk])
            if dst is qT and split:
                P.add("act", lambda e, n=n, pb=pb: e.activation(out=qT[0:64, n * 512:(n + 1) * 512], in_=pb[0:64, :],
                                                                func=AF.Copy, scale=scale),
                      writes=[("qT", n), pk])
                P.add("act", lambda e, n=n, pb=pb: e.activation(out=QB[64:128, n * 512:(n + 1) * 512],
                                                                in_=pb[64:128, :], func=AF.Copy, scale=scale),
                      writes=[("qT", n), pk])
            else:
                P.add("act", lambda e, n=n, pb=pb: e.activation(out=dst[:, n * 512:(n + 1) * 512], in_=pb,
                                                                func=AF.Copy, scale=scale),
                      writes=[(dst_key[id(dst)], n), pk])

    dst_key = {id(qT): "qT", id(kT): "kT", id(QB): "qB"}

    def transposes_to_oT(chunk, n, srcs, skeys):
        pb, pk = poolC.next()
        pbb = pb.bitcast(BF16)
        for j in range(4):
            P.add("pe", lambda e, j=j, pbb=pbb: e.transpose(out=pbb[:, j * 128:(j + 1) * 128], in_=srcs[j],
                                                            identity=identb[:]),
                  reads=[skeys[j], "identb"], writes=[pk])
        P.add("act", lambda e, pbb=pbb: e.activation(out=oT[:, chunk, n * 512:(n + 1) * 512], in_=pbb[:, 0:512],
                                                     func=AF.Copy),
              writes=[("oT", chunk, n), pk])

    poolS = Rot([bk(3), bk(4), bk(5), bk(6), bk(7)])

    def pipeline(units, stage_a, stage_b, look, with_index=False):
        nU = len(units)
        for i in range(nU + look):
            if i < nU:
                stage_a(units[i], i) if with_index else stage_a(units[i])
            if i >= look:
                stage_b(units[i - look], i - look) if with_index else stage_b(units[i - look])

    wctr = [0]

    def load_win(cols, force=None):
        s = wctr[0] % 2 if force is None else force
        wctr[0] += 1
        for i, c0 in enumerate(cols):
            wload(wsl[s][i], win_d[:, c0:c0 + 128], ("wsl", s, i), "wsl%d" % s)
        return s

    def merge(bi, wbr, first, after_first=None):
        wbrs = [rv(SB + 18560 + i * 1024, 1024, BF16, "p (k n) -> p k n", k=4) for i in range(2)]
        wgts = [rv(SB + 20608 + i * 2048, 2048, BF16, "p (k n) -> p k n", k=KD) for i in range(2)]
        sigs = [spr[i] for i in range(2)]
        tts = [rv(SB + 24704 + i * 2048, 2048, F32) for i in range(2)]
        gidx = {0: 2, 1: 0, 2: 1}[bi]
        ev = 0
        for c in range(KD):
            sl = c % 2
            wload(wbrs[sl], wbr[:, c * 128:(c + 1) * 128], ("wbr", sl), "wm%d" % sl)
            wload(wgts[sl], wgate_d[:, gidx * 1024 + c * 128:gidx * 1024 + (c + 1) * 128], ("wgt", sl), "wm%d" % sl)
            if c == 0 and after_first is not None:
                after_first()
            for n in range(4):
                py, pyk = poolA.next()
                pg, pgk = poolB.next()
                for k in range(4):
                    P.add("pe", lambda e, k=k, py=py, sl=sl, n=n: e.matmul(
                        py, lhsT=wbrs[sl][:, k, :], rhs=oT[:, k, n * 512:(n + 1) * 512],
                        start=(k == 0), stop=(k == 3)),
                        reads=[("wbr", sl), ("oT", k, n)], writes=[pyk])
                for k in range(KD):
                    P.add("pe", lambda e, k=k, pg=pg, sl=sl, n=n: e.matmul(
                        pg, lhsT=wgts[sl][:, k, :], rhs=hT[:, k, n * 512:(n + 1) * 512],
                        start=(k == 0), stop=(k == KD - 1)),
                        reads=[("wgt", sl)] + [("hT", t) for t in range(4 * n, 4 * n + 4)], writes=[pgk])
                sg = sigs[ev % 2]
                tt = tts[ev % 2]
                sgk, ttk = ("spr", ev % 2), ("tt", ev % 2)
                tal = [("E", 2 * (ev % 2)), ("E", 2 * (ev % 2) + 1)]
                ev += 1
                bcol = 32 + gidx * 8 + c
                P.add("act", lambda e, sg=sg, pg=pg, bcol=bcol: e.activation(
                    out=sg, in_=pg, func=AF.Sigmoid, bias=par[:, bcol:bcol + 1]),
                    reads=["par"], writes=[sgk, pgk])
                mslice = mergedT[:, c, n * 512:(n + 1) * 512]
                if first:
                    P.add("dve", lambda e, sg=sg, py=py, mslice=mslice: e.tensor_tensor(
                        out=mslice, in0=py, in1=sg, op=ALU.mult),
                        reads=[sgk], writes=[("mT", c, n), pyk])
                else:
                    P.add("dve", lambda e, sg=sg, py=py, tt=tt: e.tensor_tensor(
                        out=tt, in0=py, in1=sg, op=ALU.mult), reads=[sgk], writes=[ttk, pyk] + tal)
                    P.add("dve", lambda e, tt=tt, mslice=mslice: e.tensor_tensor(
                        out=mslice, in0=tt, in1=mslice, op=ALU.add), reads=[ttk] + tal, writes=[("mT", c, n)])

    for h in range(3):
        wload(wsl[0][h % 3], win_d[:, O_MQ + h * 128:O_MQ + (h + 1) * 128], ("wsl", 0, h % 3), "wsl0")
    wctr[0] += 4
    qbufs = [qT, QB]
    macc = Rot([bk(0), bk(1), bk(2), bk(3)])
    mscore = Rot([bk(4), bk(5), bk(6), bk(7)])
    munits = []
    for h in range(4):
        for n in range(4):
            accb = [macc.next(), macc.next()]
            u = len(munits)
            ets = [(Et[(2 * u + kt) % 4], ("E", (2 * u + kt) % 4)) for kt in range(2)]
            munits.append((h, n, accb, ets))
    proj_fm(qbufs[0], 0, wsl[0][0], ("wsl", 0, 0), 128.0 ** -0.5)
    wload(wsl[0][0], win_d[:, O_MQ + 3 * 128:O_MQ + 4 * 128], ("wsl", 0, 0), "wsl0")

    def m_stage_a(u):
        h, n, accb, ets = u
        qb_ = qbufs[h % 2]
        qk = "qT" if h % 2 == 0 else "qB"
        if n == 1 and h + 1 < 4:
            proj_fm(qbufs[(h + 1) % 2], 0, wsl[0][(h + 1) % 3], ("wsl", 0, (h + 1) % 3), 128.0 ** -0.5)
        for kt in range(2):
            pb, pk = mscore.next()
            E, ek = ets[kt]
            P.add("pe", lambda e, kt=kt, n=n, pb=pb, h=h, qb_=qb_: e.matmul(
                pb, lhsT=mkT[:, h, kt * 128:(kt + 1) * 128], rhs=qb_[:, n * 512:(n + 1) * 512],
                start=True, stop=True), reads=["mkT", (qk, n)], writes=[pk])
            P.add("act", lambda e, E=E, pb=pb: e.activation(out=E, in_=pb, func=AF.Exp),
                  writes=[ek, pk])

    def m_stage_b(u):
        h, n, accb, ets = u
        for j in range(4):
            ab, abk = accb[j // 2]
            a = ab[:, (j % 2) * 132:(j % 2) * 132 + 129]
            for kt in range(2):
                E, ek = ets[kt]
                P.add("pe", lambda e, a=a, E=E, j=j, kt=kt, h=h: e.matmul(
                    a, lhsT=E[:, j * 128:(j + 1) * 128], rhs=mv[:, kt, h, 0:129],
                    start=(kt == 0 and j % 2 == 0), stop=(kt == 1), skip_group_check=True),
                    reads=[ek, "mv2"], writes=[abk])
        for j in range(4):
            ab, abk = accb[j // 2]
            a = ab[:, (j % 2) * 132:(j % 2) * 132 + 129]
            rc = stat[:, 44 + j:45 + j]
            P.add("dve", lambda e, a=a, rc=rc: e.reciprocal(out=rc, in_=a[:, 128:129]),
                  writes=[("rc", j), abk])
            P.add("dve", lambda e, a=a, rc=rc, j=j: e.tensor_scalar(
                out=otokb[j], in0=a[:, 0:128], scalar1=rc, scalar2=None, op0=ALU.mult),
                reads=[("rc", j)], writes=[("otokb", j), abk])
        transposes_to_oT(h, n, otokb, [("otokb", j) for j in range(4)])

    pipeline(munits, m_stage_a, m_stage_b, 1)
    na_pre = {}

    def pre_na():
        na_pre[0] = (load_win([O_NQ, O_NK, O_NV], force=0), True)
        P.dma("pool", lambda e: e.dma_start(out=nabt_flat, in_=nab_d[0]), writes=["nabt"], stream="nab")
    merge(0, wbr_d[0], True, after_first=pre_na)
    vaugn = vaug[:, :, 0:130].rearrange("p t (h c) -> p t h c", h=2)
    P.add("dve", lambda e: e.memset(qT[64:128, :], 0.0), writes=[("qT", n) for n in range(4)])
    P.add("dve", lambda e: e.memset(QB[0:64, :], 0.0), writes=[("qT", n) for n in range(4)] + [("qB", n) for n in range(4)])
    for hp in range(4):
        if hp in na_pre:
            s = na_pre[hp][0]
        else:
            s = load_win([O_NQ + hp * 128, O_NK + hp * 128, O_NV + hp * 128], force=0)
            P.dma("pool", lambda e, hp=hp: e.dma_start(out=nabt_flat, in_=nab_d[hp]),
                  writes=["nabt"], stream="nab")
        proj_fm(qT, 0, wsl[s][0], ("wsl", s, 0), 0.125, split=True)
        proj_fm(kT, 0, wsl[s][1], ("wsl", s, 1), 1.0)
        P.add("dve", lambda e: e.memset(vaugn[:, :, :, 64:65], 1.0), writes=["vones"])
        for g in range(4):
            pb, pk = poolC.next()
            for tt_ in range(4):
                t = 4 * g + tt_
                for k in range(KD):
                    P.add("pe", lambda e, k=k, t=t, tt_=tt_, pb=pb, s=s: e.matmul(
                        pb[:, tt_ * 128:(tt_ + 1) * 128], lhsT=hT[:, k, t * 128:(t + 1) * 128],
                        rhs=wsl[s][2][:, k, :], start=(k == 0), stop=(k == KD - 1)),
                        reads=[("wsl", s, 2), ("hT", t)], writes=[pk])
            P.add("dve", lambda e, g=g, pb=pb: e.tensor_copy(
                out=vaugn[:, 4 * g:4 * g + 4, :, 0:64],
                in_=pb.rearrange("p (t h c) -> p t h c", t=4, h=2)),
                reads=["vones"], writes=[("v", g), pk])
        units = []
        ecnt = 0
        for t in range(NT):
            ab, abk = poolA.next()
            for hh in range(2):
                kts = na_key_tiles(t)
                chunks = [kts[0:4], kts[4:]] if len(kts) > 4 else [kts]
                es = []
                for ch in chunks:
                    es.append((Et[ecnt % 4], ("E", ecnt % 4)))
                    ecnt += 1
                units.append((t, hh, ab, abk, chunks, es))

        def stage_a(u):
            t, hh, ab, abk, chunks, es = u
            for ci, ch in enumerate(chunks):
                pb, pk = poolS.next()
                for i, (kt, var) in enumerate(ch):
                    P.add("pe", lambda e, i=i, kt=kt, hh=hh, t=t, pb=pb: e.matmul(
                        pb[:, i * 128:(i + 1) * 128], lhsT=kT[:, kt * 128:(kt + 1) * 128],
                        rhs=QAB[hh][:, t * 128:(t + 1) * 128], start=True, stop=False),
                        reads=[("kT", kt // 4), ("qT", t // 4)], writes=[pk])
                    P.add("pe", lambda e, i=i, var=var, hh=hh, pb=pb: e.matmul(
                        pb[:, i * 128:(i + 1) * 128], lhsT=identb[:], rhs=nabt[:, hh, var, :],
                        start=False, stop=True), reads=["identb", "nabt"], writes=[pk])
                E, ek = es[ci]
                w = 128 * len(ch)
                P.add("act", lambda e, E=E, pb=pb, w=w: e.activation(out=E[:, 0:w], in_=pb[:, 0:w], func=AF.Exp),
                      writes=[ek, pk])

        def stage_b(u, hp=hp):
            t, hh, ab, abk, chunks, es = u
            first_pv = True
            for ci, ch in enumerate(chunks):
                E, ek = es[ci]
                for i, (kt, var) in enumerate(ch):
                    last = (ci == len(chunks) - 1 and i == len(ch) - 1)
                    P.add("pe", lambda e, i=i, kt=kt, hh=hh, E=E, ab=ab, fp=first_pv, last=last: e.matmul(
                        ab[:, hh * 65:(hh + 1) * 65], lhsT=E[:, i * 128:(i + 1) * 128], rhs=vaugn[:, kt, hh, :],
                        start=(fp and hh == 0), stop=last, skip_group_check=True),
                        reads=[ek, ("v", kt // 4), "vones"], writes=[abk])
                    first_pv = False
            if hh == 1:
                rc = stat[:, 44:46]
                P.add("dve", lambda e, ab=ab: e.reciprocal(
                    out=rc, in_=ab[:, 0:130].rearrange("p (h c) -> p h c", h=2)[:, :, 64]),
                    writes=["rc2", abk])
                P.add("dve", lambda e, ab=ab, t=t: e.tensor_tensor(
                    out=otokb[t % 4].rearrange("p (h c) -> p h c", h=2),
                    in0=ab[:, 0:130].rearrange("p (h c) -> p h c", h=2)[:, :, 0:64],
                    in1=rc.unsqueeze(2).to_broadcast([128, 2, 64]), op=ALU.mult),
                    reads=["rc2"], writes=[("otokb", t % 4), abk])
                if t % 4 == 3:
                    transposes_to_oT(hp, t // 4, otokb, [("otokb", j) for j in range(4)])

        pipeline(units, stage_a, stage_b, 1)
    df_pre = {}

    def pre_df():
        df_pre[0] = load_win([O_DQ, O_DK, O_DV], force=0)
        P.dma("sp", lambda e: e.dma_start(out=alib, in_=alibi_d[0]), writes=["alib", "nabt"], stream="alibi")
    merge(1, wbr_d[1], False, after_first=pre_df)
    vt = vaug[:, :, 0:128]
    tA = rv(SB + 28800, 2048, F32)
    tB = rv(SB + 50816, 2048, F32)
    onesf = rv(SB + 52864, 512, F32)
    onesb = rv(SB + 53376, 256, BF16)
    P.add("dve", lambda e: e.memset(qT[64:128, :], 0.0), writes=[("qT", n) for n in range(4)])
    P.add("dve", lambda e: e.memset(QB[0:64, :], 0.0), writes=[("qT", n) for n in range(4)])
    P.add("dve", lambda e: e.memset(onesf, 1.0), writes=["onesf"])
    P.add("dve", lambda e: e.memset(onesb, 1.0), writes=["onesb"])
    sublnc = stat[:, 37:38]
    P.add("dve", lambda e: e.tensor_scalar(out=sublnc, in0=par[:, 440:441], scalar1=1.0 - LAM_INIT, scalar2=None,
                                           op0=ALU.mult), reads=["par"], writes=["sublnc"])
    poolD = Rot([bk(4), bk(5), bk(6)])
    spr3 = [spr[0], spr[1], rv(53248, 2048, F32)]
    for h in range(4):
        if h in df_pre:
            s = df_pre[h]
        else:
            s = load_win([O_DQ + h * 128, O_DK + h * 128, O_DV + h * 128], force=0)
            P.dma("sp", lambda e, h=h: e.dma_start(out=alib, in_=alibi_d[h]), writes=["alib"], stream="alibi")
        proj_fm(qT, 0, wsl[s][0], ("wsl", s, 0), 0.125, split=True)
        proj_fm(kT, 0, wsl[s][1], ("wsl", s, 1), 1.0)
        for g in range(4):
            pb, pk = poolC.next()
            for tt_ in range(4):
                t = 4 * g + tt_
                for k in range(KD):
                    P.add("pe", lambda e, k=k, t=t, tt_=tt_, pb=pb, s=s: e.matmul(
                        pb[:, tt_ * 128:(tt_ + 1) * 128], lhsT=hT[:, k, t * 128:(t + 1) * 128],
                        rhs=wsl[s][2][:, k, :], start=(k == 0), stop=(k == KD - 1)),
                        reads=[("wsl", s, 2), ("hT", t)], writes=[pk])
            P.add("dve", lambda e, g=g, pb=pb: e.tensor_copy(
                out=vt[:, 4 * g:4 * g + 4, :], in_=pb.rearrange("p (t c) -> p t c", t=4)),
                writes=[("v", g), pk])
        units = [(n, m, kt) for n in range(4) for m in range(2) for kt in range(NT)]
        OS = {0: (bk(0), bk(1)), 1: (bk(2), bk(3))}

        def stage_a(u, ui):
            n, m, kt = u
            pb, pk = poolD.next()
            P.add("pe", lambda e, m=m, kt=kt, n=n, pb=pb: e.matmul(
                pb, lhsT=kT[:, kt * 128:(kt + 1) * 128],
                rhs=QAB[m][:, n * 512:(n + 1) * 512], start=True, stop=True),
                reads=[("kT", kt // 4), ("qT", n)], writes=[pk])
            sp_, spk = spr3[ui % 3], ("spr", ui % 3)
            E, ek = Et[ui % 4], ("E", ui % 4)
            off = n * 512 - kt * 128 + 1920
            P.add("dve", lambda e, sp_=sp_, pb=pb, off=off: e.tensor_tensor(
                out=sp_, in0=pb, in1=alib[:, off:off + 512], op=ALU.add),
                reads=["alib"], writes=[spk, pk])
            P.add("act", lambda e, sp_=sp_, E=E: e.activation(out=E, in_=sp_, func=AF.Exp),
                  reads=[spk], writes=[ek])

        def stage_b(u, ui, h=h):
            n, m, kt = u
            E, ek = Et[ui % 4], ("E", ui % 4)
            (ob, obk), (sb, sbk) = OS[m]
            P.add("pe", lambda e, E=E, kt=kt, ob=ob: e.matmul(
                ob, lhsT=vt[:, kt, :], rhs=E, start=(kt == 0), stop=(kt == NT - 1)),
                reads=[ek, ("v", kt // 4)], writes=[obk])
            P.add("pe", lambda e, E=E, kt=kt, sb=sb: e.matmul(
                sb, lhsT=onesb, rhs=E, start=(kt == 0), stop=(kt == NT - 1)),
                reads=[ek, "onesb"], writes=[sbk])
            if m == 1 and kt == NT - 1:
                (o1, o1k), (s1, s1k) = OS[0]
                (o2, o2k), (s2, s2k) = OS[1]
                P.add("act", lambda e: e.activation(out=tA, in_=s1, func=AF.Ln), writes=["tA", s1k])
                P.add("act", lambda e: e.activation(out=tB, in_=s2, func=AF.Ln), writes=["tB", s2k])
                P.add("act", lambda e: e.activation(out=tA, in_=tA, func=AF.Exp, scale=-1.0),
                      reads=["tA"], writes=["tA"])
                P.add("act", lambda e: e.activation(out=tB, in_=tB, func=AF.Exp, scale=-1.0),
                      reads=["tB"], writes=["tB"])
                P.add("dve", lambda e: e.tensor_tensor(out=tA, in0=o1, in1=tA, op=ALU.mult),
                      reads=["tA"], writes=["tA", o1k])
                P.add("dve", lambda e: e.tensor_tensor(out=tB, in0=o2, in1=tB, op=ALU.mult),
                      reads=["tB"], writes=["tB", o2k])
                P.add("dve", lambda e: e.scalar_tensor_tensor(out=tA, in0=tB, scalar=nlam, in1=tA,
                                                              op0=ALU.mult, op1=ALU.add),
                      reads=["tB", "nlam", "tA"], writes=["tA"])
                P.add("act", lambda e: e.activation(out=tB, in_=tA, func=AF.Square), reads=["tA"], writes=["tB"])
                qb, qk = bk(7)
                P.add("pe", lambda e, qb=qb: e.matmul(qb, lhsT=onesf, rhs=tB, start=True, stop=True),
                      reads=["onesf", "tB"], writes=[qk])
                P.add("act", lambda e, qb=qb: e.activation(out=tB, in_=qb, func=AF.Ln, scale=1.0 / 128,
                                                            bias=epsb[:, 0:1]),
                      reads=["epsb"], writes=["tB", qk])
                P.add("act", lambda e: e.activation(out=tB, in_=tB, func=AF.Exp, scale=-0.5),
                      reads=["tB"], writes=["tB"])
                P.add("dve", lambda e, n=n, h=h: e.scalar_tensor_tensor(
                    out=oT[:, h, n * 512:(n + 1) * 512], in0=tA, scalar=sublnc, in1=tB,
                    op0=ALU.mult, op1=ALU.mult),
                    reads=["tA", "tB", "sublnc"], writes=[("oT", h, n)])

        pipeline(units, stage_a, stage_b, 2, with_index=True)
    wout = rv(SB, 16384, BF16, "p (k n) -> p k n", k=KD)
    wout_alias = ([("qT", n) for n in range(4)] + [("kT", n) for n in range(4)] + [("v", g) for g in range(4)]
                  + ["vones", ("wsl", 0, 0), ("wsl", 0, 1)])
    def pre_wout():
        P.dma("pool", lambda e: e.dma_start(out=wout, in_=wout_d[:, :].rearrange("(k p) n -> p k n", p=128)),
              writes=["wout"] + wout_alias, stream="wout")
    merge(2, wbr_d[2], False, after_first=pre_wout)

    oT_keys = [("oT", k, n) for k in range(4) for n in range(4)]
    wgs2 = [rv(24576 + i * 8192, 4096, BF16, "p (k n) -> p k n", k=KD) for i in range(3)]
    wus2 = [rv(24576 + i * 8192 + 4096, 4096, BF16, "p (k n) -> p k n", k=KD) for i in range(3)]
    for pi, sl in ((0, 1), (1, 2)):
        P.dma("pool", lambda e, pi=pi, sl=sl: e.dma_start(
            out=wgs2[sl], in_=w2g[:, pi * 256:(pi + 1) * 256].rearrange("(k p) n -> p k n", p=128)),
            writes=[("wg", sl)] + oT_keys, stream="wgu%d" % sl)
        P.dma("pool", lambda e, pi=pi, sl=sl: e.dma_start(
            out=wus2[sl], in_=w2u[:, pi * 256:(pi + 1) * 256].rearrange("(k p) n -> p k n", p=128)),
            writes=[("wu", sl)] + oT_keys, stream="wgu%d" % sl)
    pop2 = Rot([bk(4), bk(5), bk(6), bk(7)])
    for t in range(NT):
        for j in range(2):
            po, pok = pop2.next()
            for k in range(KD):
                P.add("pe", lambda e, k=k, po=po, t=t, j=j: e.matmul(
                    po, lhsT=mergedT[:, k, t * 128:(t + 1) * 128], rhs=wout[:, k, j * 512:(j + 1) * 512],
                    start=(k == 0), stop=(k == KD - 1)),
                    reads=["wout", ("mT", k, t // 4)], writes=[pok])
            P.add("dve", lambda e, po=po, t=t, j=j: e.tensor_tensor(
                out=xs[:, t, j * 512:(j + 1) * 512], in0=po, in1=xs[:, t, j * 512:(j + 1) * 512], op=ALU.add),
                writes=[("xs", t), pok])

    norm_to_hT(16, hb_f, junk_f, Rot([bkb(6), bkb(7)]))
    ffn(w2g, w2u, w2d, "f2", pidx0=1, preloaded=2)

    gfin = rv(83968, 4096, F32)
    outb = [rv(88064 + i * 4096, 4096, F32) for i in range(4)]
    P.dma("sp", lambda e: e.dma_start(out=gfin, in_=gfin_d[:, :]), reads=[("xs", 0)], writes=["gfin"], stream="gfin")
    junk_o = rv(104448, 2048, BF16)
    for g in range(4):
        for t in range(4 * g, 4 * g + 4):
            P.add("act", lambda e, t=t: e.activation(out=junk_o, in_=xs[:, t, :], func=AF.Square,
                                                      accum_out=ss[:, t:t + 1]),
                  reads=[("xs", t)], writes=["junk", ("ss", t)])
        rstd_ops(ss[:, 4 * g:4 * g + 4], rstd[:, 4 * g:4 * g + 4], D, [("ss", u) for u in range(4 * g, 4 * g + 4)],
                 ("rstd", g), 4)
        for t in range(4 * g, 4 * g + 4):
            ob = outb[t % 4]
            P.add("dve", lambda e, t=t, ob=ob: e.scalar_tensor_tensor(
                out=ob, in0=xs[:, t, :], scalar=rstd[:, t:t + 1], in1=gfin, op0=ALU.mult, op1=ALU.mult),
                reads=[("xs", t), ("rstd", g), "gfin"], writes=[("ob", t % 4)])
            P.dma("sp", lambda e, t=t, ob=ob: e.dma_start(out=y_d[t * 128:(t + 1) * 128, :], in_=ob),
                  reads=[("ob", t % 4)], stream="out%d" % (t % 4))
    counts = P.emit(final_wait_streams=["out0", "out1", "out2", "out3"])
    return nc, counts


_CACHE = {}


def kernel(x, mem, ffn1_norm, ffn1_w_gate, ffn1_w_up, ffn1_w_down, mix_norm, w_in, na_rpb,
           diff_lambda_q1, diff_lambda_k1, diff_lambda_q2, diff_lambda_k2, diff_subln,
           mem_norm, w_mem_kv, w_gate, b_gate, w_br_na, w_br_diff, w_br_mem, w_out,
           ffn2_norm, ffn2_w_gate, ffn2_w_up, ffn2_w_down, final_norm):
    f = lambda a: np.ascontiguousarray(np.asarray(a, dtype=np.float32))
    x = f(x)
    mem = f(mem)
    colT = lambda v: f(v).reshape(-1, 128).T
    rep = lambda v: np.broadcast_to(f(v).reshape(1, -1), (128, f(v).size))
    par = np.concatenate([
        colT(ffn1_norm[0]), colT(mix_norm[0]), colT(ffn2_norm[0]), colT(mem_norm[0]), colT(b_gate[0]),
        rep(diff_lambda_q1[0]), rep(diff_lambda_k1[0]), rep(diff_lambda_q2[0]), rep(diff_lambda_k2[0]),
        rep(diff_subln[0]), colT(diff_subln[0])], axis=1)
    par = np.ascontiguousarray(par, dtype=np.float32)
    assert par.shape == (128, 441)
    gfin = np.ascontiguousarray(rep(final_norm))
    shared = {
        "ffn1_w_gate": f(ffn1_w_gate)[0], "ffn1_w_up": f(ffn1_w_up)[0], "ffn1_w_down": f(ffn1_w_down)[0],
        "ffn2_w_gate": f(ffn2_w_gate)[0], "ffn2_w_up": f(ffn2_w_up)[0], "ffn2_w_down": f(ffn2_w_down)[0],
        "w_in": f(w_in)[0], "w_mem_kv": f(w_mem_kv)[0], "w_gate": f(w_gate)[0],
        "w_br_mem": f(w_br_mem)[0], "w_br_na": f(w_br_na)[0], "w_br_diff": f(w_br_diff)[0],
        "w_out": f(w_out)[0], "params": par, "gfin": gfin,
        "ident": np.eye(128, dtype=np.float32), "alibi": build_alibi(),
        "nab": np.ascontiguousarray(build_nab(f(na_rpb)[0]).reshape(4, 2, 21, 128, 128).transpose(0, 3, 1, 2, 4)
                                    .reshape(4, 128, 2 * 21 * 128)),
    }
    if "nc" not in _CACHE:
        _CACHE["nc"] = build_nc()[0]
    nc = _CACHE["nc"]
    in_maps = []
    for b in range(8):
        m = dict(shared)
        m["x"] = x[b]
        m["mem"] = mem[b]
        in_maps.append(m)
    res = run_bass_kernel_spmd(nc, in_maps, core_ids=list(range(8)))
    return np.stack([np.asarray(r["y"], dtype=np.float32) for r in res.results], axis=0)
```

```python
import contextlib
import numpy as np
import concourse.bass as bass
import concourse.mybir as mybir
from concourse.bass_utils import run_bass_kernel_spmd

F32 = mybir.dt.float32
BF16 = mybir.dt.bfloat16
AF = mybir.ActivationFunctionType
ALU = mybir.AluOpType

S = 2048
D = 1024
NT = 16
KD = 8
DFF = 2816
NCH = 22
MEMT = 256
EPS = 1e-6
LAM_INIT = 0.2
SLOPES = [2.0 ** (-8.0 * (i + 1) / 4) for i in range(4)]
O_NQ, O_NK, O_NV, O_DQ, O_DK, O_DV, O_MQ = 0, 512, 1024, 1536, 2048, 2560, 3072
ALW = 3968
MASKV = -30000.0
SAME_ENGINE_SYNC = True
ATTACH_WAIT = True

ENGINES = ("pe", "act", "dve", "pool", "sp")


class Op:
    __slots__ = ("idx", "eng", "fn", "reads", "writes", "is_dma", "stream", "deps",
                 "signal", "count", "deps_x")

    def __init__(self, idx, eng, fn, reads, writes, is_dma, stream):
        self.idx = idx
        self.eng = eng
        self.fn = fn
        self.reads = reads
        self.writes = writes
        self.is_dma = is_dma
        self.stream = stream
        self.deps = set()
        self.deps_x = set()
        self.signal = False
        self.count = 0


class Prog:
    def __init__(self, nc, same_engine_sync=True):
        self.nc = nc
        self.ops = []
        self.last_writer = {}
        self.readers = {}
        self.same_engine_sync = same_engine_sync
        self.stream_names = []
        self.last_of_eng = {}
        self.last_of_stream = {}
        self.pending = {}

    def fence(self):
        snap = set(self.last_of_eng.values()) | set(self.last_of_stream.values())
        for e in ENGINES:
            self.pending[e] = set(snap) | self.pending.get(e, set())

    def _add(self, eng, fn, reads, writes, is_dma, stream):
        op = Op(len(self.ops), eng, fn, tuple(reads), tuple(writes), is_dma, stream)
        deps = set()
        deps_x = set()
        for r in op.reads:
            lw = self.last_writer.get(r)
            if lw is not None:
                deps.add(lw)
        for w in op.writes:
            tgt = deps_x if (isinstance(w, tuple) and w[0] == "PS") else deps
            lw = self.last_writer.get(w)
            if lw is not None:
                tgt.add(lw)
            for rd in self.readers.get(w, ()):
                tgt.add(rd)
        pend = self.pending.pop(eng, None)
        if pend:
            deps_x |= pend
        deps.discard(op.idx)
        deps_x.discard(op.idx)
        op.deps = deps
        op.deps_x = deps_x - deps
        for r in op.reads:
            self.readers.setdefault(r, []).append(op.idx)
        for w in op.writes:
            self.last_writer[w] = op.idx
            self.readers[w] = []
        self.ops.append(op)
        if is_dma:
            self.last_of_stream[stream] = op.idx
        else:
            self.last_of_eng[eng] = op.idx
        return op

    def add(self, eng, fn, reads=(), writes=()):
        return self._add(eng, fn, reads, writes, False, None)

    def dma(self, eng, fn, reads=(), writes=(), stream=None):
        if stream not in self.stream_names:
            self.stream_names.append(stream)
        return self._add(eng, fn, reads, writes, True, stream)

    def emit(self, final_wait_streams=()):
        nc = self.nc
        ops = self.ops

        def skip_dep(op, dop, is_x):
            if dop.is_dma or op.is_dma:
                return False
            if dop.eng != op.eng:
                return False
            if is_x or dop.eng == "pe" or not self.same_engine_sync:
                return True
            return False

        for op in ops:
            latest = {}
            for dset, is_x in ((op.deps, False), (op.deps_x, True)):
                for d in dset:
                    dop = ops[d]
                    if not dop.is_dma and not skip_dep(op, dop, is_x):
                        if latest.get(dop.eng, -1) < d:
                            latest[dop.eng] = d
            op.deps = set(d for d in op.deps if ops[d].is_dma)
            op.deps_x = set(d for d in op.deps_x if ops[d].is_dma)
            for d in latest.values():
                ops[d].signal = True
                op.deps_x.add(d)
        self_skip = skip_dep

        def skip_dep(op, dop, is_x):
            return False
        eng_count = {e: 0 for e in ENGINES}
        stream_count = {s: 0 for s in self.stream_names}
        stream_hist = {s: [] for s in self.stream_names}
        for op in ops:
            if op.is_dma:
                stream_count[op.stream] += 16
                op.count = stream_count[op.stream]
                stream_hist[op.stream].append((op.idx, op.count))
            elif op.signal:
                eng_count[op.eng] += 1
                op.count = eng_count[op.eng]
        with contextlib.ExitStack() as es:
            eng_sem = {e: es.enter_context(nc.semaphore("s_" + e)) for e in ENGINES}
            st_sem = {s: es.enter_context(nc.semaphore("d_" + str(i)))
                      for i, s in enumerate(self.stream_names)}
            block = es.enter_context(nc.Block())
            per_eng = {e: [op for op in ops if op.eng == e] for e in ENGINES}

            def stream_value_before(stream, idx):
                v = 0
                for (i, c) in stream_hist[stream]:
                    if i < idx:
                        v = c
                    else:
                        break
                return v

            vc = [None] * len(ops)
            know = {e: {} for e in ENGINES}
            last_sig = {e: 0 for e in ENGINES}
            waits_of = [None] * len(ops)

            def merge_into(dst, src):
                for k, v in src.items():
                    if dst.get(k, 0) < v:
                        dst[k] = v

            for op in ops:
                K = know[op.eng]
                need = {}
                for d in list(op.deps) + list(op.deps_x):
                    dop = ops[d]
                    if dop.is_dma:
                        key = ("d", dop.stream)
                        val = stream_value_before(dop.stream, op.idx)
                    else:
                        key = ("e", dop.eng)
                        val = dop.count
                    if need.get(key, (0, None))[0] < val:
                        need[key] = (val, d)
                wl = []
                for key, (val, d) in sorted(need.items(), key=lambda kv: -kv[1][1]):
                    if K.get(key, 0) >= val:
                        continue
                    wl.append((key, val))
                    if K.get(key, 0) < val:
                        K[key] = val
                    merge_into(K, vc[d])
                waits_of[op.idx] = wl
                v = dict(K)
                if op.is_dma:
                    v[("d", op.stream)] = max(v.get(("d", op.stream), 0), op.count)
                else:
                    if op.signal:
                        last_sig[op.eng] = op.count
                    v[("e", op.eng)] = max(v.get(("e", op.eng), 0), last_sig[op.eng])
                    if not self.same_engine_sync or op.eng == "pe":
                        pass
                vc[op.idx] = v
            self.n_waits = sum(len(w) for w in waits_of)

            def run(engname, eng):
                for op in per_eng[engname]:
                    wl = waits_of[op.idx]
                    attach = None
                    if ATTACH_WAIT and wl and not op.is_dma:
                        attach = wl[-1]
                        wl = wl[:-1]
                    for key, val in wl:
                        sem = st_sem[key[1]] if key[0] == "d" else eng_sem[key[1]]
                        eng.wait_ge(sem, val)
                    ins = op.fn(eng)
                    if attach is not None:
                        key, val = attach
                        sem = st_sem[key[1]] if key[0] == "d" else eng_sem[key[1]]
                        ins._wait_ge(sem, val)
                    if op.is_dma:
                        ins.then_inc(st_sem[op.stream], 16)
                    elif op.signal:
                        ins.then_inc(eng_sem[op.eng], 1)
                if engname == "sp":
                    for s in final_wait_streams:
                        eng.wait_ge(st_sem[s], stream_count[s])

            block.tensor(lambda e: run("pe", e))
            block.scalar(lambda e: run("act", e))
            block.vector(lambda e: run("dve", e))
            block.gpsimd(lambda e: run("pool", e))
            block.sync(lambda e: run("sp", e))
        return eng_count, stream_count


class Rot:
    def __init__(self, items):
        self.items = items
        self.i = 0

    def next(self):
        it = self.items[self.i % len(self.items)]
        self.i += 1
        return it


def na_key_tiles(t):
    if t <= 1:
        return [(kt, 5 + 4 * t + kt) for kt in range(4)]
    if t >= 14:
        return [(kt, 13 + 4 * (t - 14) + (kt - 12)) for kt in range(12, 16)]
    return [(kt, kt - t + 2) for kt in range(t - 2, t + 3)]


def na_variant_reps():
    reps = {}
    for t in (5, 0, 1, 14, 15):
        for kt, v in na_key_tiles(t):
            reps[v] = (t, kt)
    return [reps[v] for v in range(21)]


def build_nab(rpb):
    out = np.empty((8, 21, 128, 128), np.float32)
    idx = np.arange(128)
    for v, (t, kt) in enumerate(na_variant_reps()):
        qr = 2 * t + idx // 64
        qc = idx % 64
        kr = 2 * kt + idx // 64
        kc = idx % 64
        rs = np.clip(qr - 4, 0, 24)
        cs = np.clip(qc - 8, 0, 48)
        KR, QR = kr[:, None], qr[None, :]
        KC, QC = kc[:, None], qc[None, :]
        inside = (KR >= rs[None, :]) & (KR < rs[None, :] + 8) & (KC >= cs[None, :]) & (KC < cs[None, :] + 16)
        dr = np.clip(KR - QR + 7, 0, 14)
        dc = np.clip(KC - QC + 15, 0, 30)
        g = rpb[:, dr, dc]
        out[:, v] = np.where(inside[None], g, np.float32(MASKV))
    return out


def build_alibi():
    k = np.arange(128, dtype=np.float64)[:, None]
    u = np.arange(ALW, dtype=np.float64)[None, :] - 1920.0
    dist = np.abs(u - k)
    return np.stack([(-s * dist).astype(np.float32) for s in SLOPES])


def build_nc():
    nc = bass.Bass("TRN2", target_bir_lowering=False)

    def din(name, shape):
        return nc.dram_tensor(name, list(shape), F32, kind="ExternalInput").ap()

    x_d = din("x", [S, D])
    mem_d = din("mem", [MEMT, D])
    w1g, w1u, w1d = din("ffn1_w_gate", [D, DFF]), din("ffn1_w_up", [D, DFF]), din("ffn1_w_down", [DFF, D])
    w2g, w2u, w2d = din("ffn2_w_gate", [D, DFF]), din("ffn2_w_up", [D, DFF]), din("ffn2_w_down", [DFF, D])
    win_d = din("w_in", [D, 3584])
    wkv_d = din("w_mem_kv", [D, 1024])
    wgate_d = din("w_gate", [D, 3072])
    wbr_d = [din("w_br_mem", [512, D]), din("w_br_na", [512, D]), din("w_br_diff", [512, D])]
    wout_d = din("w_out", [D, D])
    NPAR = 441
    par_d = din("params", [128, NPAR])
    gfin_d = din("gfin", [128, D])
    ident_d = din("ident", [128, 128])
    alibi_d = din("alibi", [4, 128, ALW])
    nab_d = din("nab", [4, 128, 2 * 21 * 128])
    y_d = nc.dram_tensor("y", [S, D], F32, kind="ExternalOutput").ap()

    xs = nc.alloc_sbuf_tensor("xs", [128, NT, D], F32)
    hT = nc.alloc_sbuf_tensor("hT", [128, KD, S], BF16)
    par = nc.alloc_sbuf_tensor("par", [128, NPAR], F32)
    identb = nc.alloc_sbuf_tensor("identb", [128, 128], BF16)
    stat = nc.alloc_sbuf_tensor("stat", [128, 64], F32)
    RB = 110592
    R = nc.alloc_sbuf_tensor("R", [128, RB // 4], F32)

    def rv(off, nbytes, dt, pat=None, **kw):
        assert off % 4 == 0 and nbytes % 4 == 0 and off + nbytes <= RB
        v = R[:, off // 4:(off + nbytes) // 4]
        if dt is not F32:
            v = v.bitcast(dt)
        if pat:
            v = v.rearrange(pat, **kw)
        return v

    banks = [nc.alloc_psum_tensor("B%d" % i, [128, 512], F32) for i in range(8)]

    def bk(i):
        return banks[i][:, :], ("PS", i)

    def bkb(i):
        return banks[i][:, :].bitcast(BF16), ("PS", i)

    P = Prog(nc, same_engine_sync=SAME_ENGINE_SYNC)

    for i in range(4):
        P.dma("sp", lambda e, i=i: e.dma_start(
            out=xs[:, 4 * i:4 * i + 4, :],
            in_=x_d[512 * i:512 * (i + 1), :].rearrange("(t p) d -> p t d", p=128)),
            writes=[("xs", t) for t in range(4 * i, 4 * i + 4)], stream="x%d" % i)
    P.dma("sp", lambda e: e.dma_start(out=par[:], in_=par_d[:, :]), writes=["par"], stream="par")
    P.dma("pool", lambda e: e.dma_start(out=identb[:], in_=ident_d[:, :]), writes=["identb"], stream="ident")

    def wload(dst, src2d, key, stream, pat="(k p) n -> p k n", alias=()):
        P.dma("pool", lambda e: e.dma_start(out=dst, in_=src2d.rearrange(pat, p=128)),
              writes=[key] + list(alias), stream=stream)

    rstd_ctr = [0]

    def rstd_ops(ss_ap, out_ap, n, rkeys, wkey, width):
        assert width <= 4
        c0 = 48 + (rstd_ctr[0] % 4) * 4
        rstd_ctr[0] += 1
        tmp = stat[:, c0:c0 + width]
        tk = ("stat_tmp", c0)
        P.add("act", lambda e: e.activation(out=tmp, in_=ss_ap, func=AF.Ln, scale=1.0 / n, bias=epsb[:, 0:1]),
              reads=list(rkeys) + ["epsb"], writes=[tk])
        P.add("act", lambda e: e.activation(out=out_ap, in_=tmp, func=AF.Exp, scale=-0.5),
              reads=[tk], writes=[wkey])

    epsb = nc.alloc_sbuf_tensor("epsb", [128, 1], F32)
    P.add("dve", lambda e: e.memset(epsb[:], EPS), writes=["epsb"])

    ss = stat[:, 0:16]
    rstd = stat[:, 16:32]

    def norm_to_hT(gcol, hb, junk, trpool):
        for t in range(NT):
            P.add("act", lambda e, t=t: e.activation(out=junk, in_=xs[:, t, :], func=AF.Square,
                                                      accum_out=ss[:, t:t + 1]),
                  reads=[("xs", t)], writes=["junk", ("ss", t)])
            if t % 4 == 3:
                g0 = t - 3
                rstd_ops(ss[:, g0:g0 + 4], rstd[:, g0:g0 + 4], D, [("ss", u) for u in range(g0, g0 + 4)],
                         ("rstd", g0 // 4), 4)
        for t in range(NT):
            hbt = hb[t % 2]
            hk = ("hb", t % 2)
            P.add("act", lambda e, t=t, hbt=hbt: e.activation(
                out=hbt, in_=xs[:, t, :], func=AF.Copy, scale=rstd[:, t:t + 1]),
                reads=[("xs", t), ("rstd", t // 4)], writes=[hk])
            pb, pk = trpool.next()
            for k in range(KD):
                P.add("pe", lambda e, k=k, hbt=hbt, pb=pb: e.transpose(
                    out=pb[:, k * 128:(k + 1) * 128], in_=hbt[:, k * 128:(k + 1) * 128], identity=identb[:]),
                    reads=[hk, "identb"], writes=[pk])
            P.add("dve", lambda e, t=t, pb=pb: e.tensor_tensor(
                out=hT[:, :, t * 128:(t + 1) * 128],
                in0=pb[:, 0:1024].rearrange("p (k n) -> p k n", k=KD),
                in1=par[:, gcol:gcol + KD].unsqueeze(2).to_broadcast([128, KD, 128]),
                op=ALU.mult),
                reads=["par"], writes=[("hT", t), pk])

    WG_ALIAS = {0: [("mT", c, n) for c in (6, 7) for n in range(4)],
                1: [("oT", k, n) for k in (0, 1) for n in range(4)],
                2: [("oT", k, n) for k in (2, 3) for n in range(4)]}
    WD_ALIAS = {0: ["wout", ("spr", 2)] + [("qT", n) for n in range(4)] + [("qB", n) for n in range(4)],
                1: ["wout", ("wsl", 0, 0), ("wsl", 0, 1), ("wsl", 0, 2)]}

    def ffn(wg_d, wu_d, wd_d, tag, hook_a=None, hook_b=None, hook_c=None, pidx0=0, preloaded=0):
        actT = rv(0, 24576, BF16, "p (c n) -> p c n", c=6)
        wgs = [rv(24576 + i * 8192, 4096, BF16, "p (k n) -> p k n", k=KD) for i in range(3)]
        wus = [rv(24576 + i * 8192 + 4096, 4096, BF16, "p (k n) -> p k n", k=KD) for i in range(3)]
        wds = [rv(49152 + i * 12288, 12288, BF16, "p (c n) -> p c n", c=6) for i in range(2)]
        sgs = [rv(77824 + i * 2048, 2048, F32) for i in range(2)]
        pgp = Rot([bk(0), bk(1)])
        pup = Rot([bk(2), bk(3)])
        pop = Rot([bk(4), bk(5)])
        groups = [[0, 1, 2], [3, 4, 5], [6, 7, 8], [9, 10]]
        pidx = pidx0
        npl = 0
        ev = 0
        for gi, grp in enumerate(groups):
            ncg = 2 * len(grp)
            c0 = 2 * grp[0]
            wd = wds[gi % 2]
            for pi in grp:
                sl = pidx % 3
                pidx += 1
                if npl >= preloaded:
                    wload(wgs[sl], wg_d[:, pi * 256:(pi + 1) * 256], ("wg", sl), "wgu%d" % sl, alias=WG_ALIAS[sl])
                    wload(wus[sl], wu_d[:, pi * 256:(pi + 1) * 256], ("wu", sl), "wgu%d" % sl, alias=WG_ALIAS[sl])
                npl += 1
                if pi == grp[0]:
                    wload(wd[:, 0:ncg, :], wd_d[c0 * 128:(c0 + ncg) * 128, :], ("wd", gi % 2), "wd%d" % (gi % 2),
                          pat="(c p) n -> p c n", alias=WD_ALIAS[gi % 2])
                for cc in range(2):
                    cl = (pi - grp[0]) * 2 + cc
                    for n in range(4):
                        pg, pgk = pgp.next()
                        pu, puk = pup.next()
                        hkeys = [("hT", t) for t in range(4 * n, 4 * n + 4)]
                        for k in range(KD):
                            P.add("pe", lambda e, k=k, pg=pg, sl=sl, cc=cc, n=n: e.matmul(
                                pg, lhsT=wgs[sl][:, k, cc * 128:(cc + 1) * 128],
                                rhs=hT[:, k, n * 512:(n + 1) * 512], start=(k == 0), stop=(k == KD - 1)),
                                reads=[("wg", sl)] + hkeys, writes=[pgk])
                        for k in range(KD):
                            P.add("pe", lambda e, k=k, pu=pu, sl=sl, cc=cc, n=n: e.matmul(
                                pu, lhsT=wus[sl][:, k, cc * 128:(cc + 1) * 128],
                                rhs=hT[:, k, n * 512:(n + 1) * 512], start=(k == 0), stop=(k == KD - 1)),
                                reads=[("wu", sl)] + hkeys, writes=[puk])
                        sg = sgs[ev % 2]
                        sgk = ("sg", ev % 2)
                        ev += 1
                        P.add("act", lambda e, sg=sg, pg=pg: e.activation(out=sg, in_=pg, func=AF.Silu),
                              writes=[sgk, pgk])
                        P.add("dve", lambda e, sg=sg, pu=pu, cl=cl, n=n: e.tensor_tensor(
                            out=actT[:, cl, n * 512:(n + 1) * 512], in0=pu, in1=sg, op=ALU.mult),
                            reads=[sgk], writes=[("actT", cl, n), puk])
                if hook_a is not None and gi == 0 and pi == grp[0]:
                    hook_a()
            if hook_b is not None and gi == 0:
                hook_b()
            for t in range(NT):
                for j in range(2):
                    po, pok = pop.next()
                    for c in range(ncg):
                        P.add("pe", lambda e, c=c, po=po, t=t, j=j, wd=wd, ncg=ncg: e.matmul(
                            po, lhsT=actT[:, c, t * 128:(t + 1) * 128], rhs=wd[:, c, j * 512:(j + 1) * 512],
                            start=(c == 0), stop=(c == ncg - 1)),
                            reads=[("actT", c, t // 4), ("wd", gi % 2)], writes=[pok])
                    P.add("dve", lambda e, po=po, t=t, j=j: e.scalar_tensor_tensor(
                        out=xs[:, t, j * 512:(j + 1) * 512], in0=po, scalar=0.5,
                        in1=xs[:, t, j * 512:(j + 1) * 512], op0=ALU.mult, op1=ALU.add),
                        writes=[("xs", t), pok])
            if hook_c is not None and gi == 0:
                hook_c()


    memraw = rv(83968, 8192, F32, "p (t d) -> p t d", t=2)
    memnb = rv(92160, 4096, BF16, "p (t d) -> p t d", t=2)
    memnT = rv(96256, 4096, BF16, "p (k n) -> p k n", k=KD)
    wkc = [rv(100352, 4096, BF16, "p (k n) -> p k n", k=KD),
           rv(83968, 4096, BF16, "p (k n) -> p k n", k=KD),
           rv(88064, 4096, BF16, "p (k n) -> p k n", k=KD),
           rv(92160, 4096, BF16, "p (k n) -> p k n", k=KD)]
    mv = rv(104448, 2112, BF16, "p (t h c) -> p t h c", t=2, h=4)
    mkT = rv(106560, 2048, BF16, "p (h n) -> p h n", h=4)
    ssm = stat[:, 38:40]
    rsm = stat[:, 40:42]

    def mem_prep_a():
        P.dma("sp", lambda e: e.dma_start(out=memraw, in_=mem_d.rearrange("(t p) d -> p t d", p=128)),
              writes=["memraw"], stream="mem")
        wload(wkc[0], wkv_d[:, 0:256], ("wkc", 0), "wkc0")
        for t in range(2):
            P.add("act", lambda e, t=t: e.activation(out=junk_f, in_=memraw[:, t, :], func=AF.Square,
                                                      accum_out=ssm[:, t:t + 1]),
                  reads=["memraw"], writes=["junk", ("ssm", t)])
        rstd_ops(ssm, rsm, D, [("ssm", 0), ("ssm", 1)], "rsm", 2)
        for t in range(2):
            P.add("dve", lambda e, t=t: e.tensor_scalar(out=memnb[:, t, :], in0=memraw[:, t, :],
                                                        scalar1=rsm[:, t:t + 1], scalar2=None, op0=ALU.mult),
                  reads=["memraw", "rsm"], writes=[("memnb", t)])

    def mem_prep_b():
        P.dma("pool", lambda e: e.dma_start(out=wkc[1], in_=wkv_d[:, 256:512].rearrange("(k p) n -> p k n", p=128)),
              writes=[("wkc", 1), "memraw"], stream="wkc1")
        P.dma("pool", lambda e: e.dma_start(out=wkc[2], in_=wkv_d[:, 512:768].rearrange("(k p) n -> p k n", p=128)),
              writes=[("wkc", 2), "memraw"], stream="wkc2")
        for t in range(2):
            pb, pk = bkb(6 + t)
            for k in range(KD):
                P.add("pe", lambda e, k=k, t=t, pb=pb: e.transpose(
                    out=pb[:, k * 128:(k + 1) * 128], in_=memnb[:, t, k * 128:(k + 1) * 128], identity=identb[:]),
                    reads=[("memnb", t), "identb"], writes=[pk])
            P.add("dve", lambda e, t=t, pb=pb: e.tensor_tensor(
                out=memnT[:, :, t * 128:(t + 1) * 128],
                in0=pb[:, 0:1024].rearrange("p (k n) -> p k n", k=KD),
                in1=par[:, 24:32].unsqueeze(2).to_broadcast([128, KD, 128]), op=ALU.mult),
                reads=["par"], writes=["memnT", pk])
        P.dma("pool", lambda e: e.dma_start(out=wkc[3], in_=wkv_d[:, 768:1024].rearrange("(k p) n -> p k n", p=128)),
              writes=[("wkc", 3), ("memnb", 0), ("memnb", 1)], stream="wkc3")

    def mem_prep_c():
        P.add("dve", lambda e: e.memset(mv[:, :, :, 128:129], 1.0), writes=["mv"])
        mpool = Rot([bk(6), bk(7)])
        for ci in range(4):
            sl = ci
            if ci < 2:
                for hh in range(2):
                    h = 2 * ci + hh
                    pb, pk = mpool.next()
                    for k in range(KD):
                        P.add("pe", lambda e, k=k, hh=hh, pb=pb, sl=sl: e.matmul(
                            pb[:, 0:256], lhsT=wkc[sl][:, k, hh * 128:(hh + 1) * 128], rhs=memnT[:, k, :],
                            start=(k == 0), stop=(k == KD - 1)),
                            reads=[("wkc", sl), "memnT"], writes=[pk])
                    P.add("act", lambda e, h=h, pb=pb: e.activation(out=mkT[:, h, :], in_=pb[:, 0:256], func=AF.Copy),
                          writes=["mkT", pk])
            else:
                for t in range(2):
                    pb, pk = mpool.next()
                    for k in range(KD):
                        P.add("pe", lambda e, k=k, t=t, pb=pb, sl=sl: e.matmul(
                            pb[:, 0:256], lhsT=memnT[:, k, t * 128:(t + 1) * 128], rhs=wkc[sl][:, k, :],
                            start=(k == 0), stop=(k == KD - 1)),
                            reads=[("wkc", sl), "memnT"], writes=[pk])
                    h0 = 2 * (ci - 2)
                    P.add("dve", lambda e, t=t, pb=pb, h0=h0: e.tensor_copy(
                        out=mv[:, t, h0:h0 + 2, 0:128], in_=pb[:, 0:256].rearrange("p (h c) -> p h c", h=2)),
                        reads=["mv"], writes=["mv2", pk])

    hb_f = [rv(73728, 2048, BF16), rv(75776, 2048, BF16)]
    junk_f = rv(81920, 2048, BF16)
    trp = Rot([bkb(6), bkb(7)])

    norm_to_hT(0, hb_f, junk_f, trp)
    ffn(w1g, w1u, w1d, "f1", hook_a=mem_prep_a, hook_b=mem_prep_b, hook_c=mem_prep_c)
    norm_to_hT(8, hb_f, junk_f, Rot([bkb(6), bkb(7)]))
    P.fence()

    mergedT = rv(0, 32768, BF16, "p (k n) -> p k n", k=KD)
    oT = rv(32768, 16384, BF16, "p (k n) -> p k n", k=4)
    hb_m = [rv(49152, 2048, BF16), rv(51200, 2048, BF16)]
    junk_m = rv(53248, 2048, BF16)
    SB = 55296
    qT = rv(SB, 4096, BF16)
    kT = rv(SB + 4096, 4096, BF16)
    vaug = rv(SB + 8192, 4224, BF16, "p (t c) -> p t c", t=NT)
    wsl = [[rv(SB + 12416 + s * 6144 + i * 2048, 2048, BF16, "p (k n) -> p k n", k=KD) for i in range(3)]
           for s in range(2)]
    Et = [rv(SB + 24704 + i * 1024, 1024, BF16) for i in range(4)]
    otok = rv(SB + 28800, 1024, F32)
    otokb = [rv(SB + 29824 + i * 256, 256, BF16) for i in range(4)]
    spr = [rv(SB + 30848 + i * 2048, 2048, F32) for i in range(2)]
    XO = SB + 34944
    alib = rv(XO, 15872, F32)
    nabt = rv(XO, 10752, BF16, "p (h v q) -> p h v q", h=2, v=21)
    nabt_flat = rv(XO, 10752, BF16)

    poolA = Rot([bk(0), bk(1), bk(2)])
    poolB = Rot([bk(3), bk(4), bk(5)])
    poolC = Rot([bk(6), bk(7)])


    lamt = stat[:, 32:36]
    P.add("dve", lambda e: e.tensor_tensor(out=otok[:, 0:64], in0=par[:, 56:120], in1=par[:, 120:184], op=ALU.mult),
          reads=["par"], writes=["otok"])
    P.add("dve", lambda e: e.reduce_sum(out=lamt[:, 0:1], in_=otok[:, 0:64], axis=mybir.AxisListType.X),
          reads=["otok"], writes=["lam0"])
    P.add("dve", lambda e: e.tensor_tensor(out=otok[:, 64:128], in0=par[:, 184:248], in1=par[:, 248:312], op=ALU.mult),
          reads=["par"], writes=["otok2"])
    P.add("dve", lambda e: e.reduce_sum(out=lamt[:, 1:2], in_=otok[:, 64:128], axis=mybir.AxisListType.X),
          reads=["otok2"], writes=["lam1"])
    P.add("act", lambda e: e.activation(out=lamt[:, 2:4], in_=lamt[:, 0:2], func=AF.Exp),
          reads=["lam0", "lam1"], writes=["lam2"])
    nlam = stat[:, 36:37]
    P.add("dve", lambda e: e.scalar_tensor_tensor(out=nlam, in0=lamt[:, 3:4], scalar=-LAM_INIT, in1=lamt[:, 2:3],
                                                  op0=ALU.add, op1=ALU.subtract),
          reads=["lam2"], writes=["nlam"])
    QB = rv(49152, 4096, BF16)
    QAB = [qT, QB]

    def proj_fm(dst, col0, slot_ap, wkey, scale, split=False):
        for n in range(4):
            pb, pk = poolC.next()
            for k in range(KD):
                P.add("pe", lambda e, k=k, n=n, pb=pb: e.matmul(
                    pb, lhsT=slot_ap[:, k, :], rhs=hT[:, k, n * 512:(n + 1) * 512],
                    start=(k == 0), stop=(k == KD - 1)),
                    reads=[wkey] + [("hT", t) for t in range(4 * n, 4 * n + 4)], writes=[pk])
            if dst is qT and split:
                P.add("act", lambda e, n=n, pb=pb: e.activation(out=qT[0:64, n * 512:(n + 1) * 512], in_=pb[0:64, :],
                                                                func=AF.Copy, scale=scale),
                      writes=[("qT", n), pk])
                P.add("act", lambda e, n=n, pb=pb: e.activation(out=QB[64:128, n * 512:(n + 1) * 512],
                                                                in_=pb[64:128, :], func=AF.Copy, scale=scale),
                      writes=[("qT", n), pk])
            else:
                P.add("act", lambda e, n=n, pb=pb: e.activation(out=dst[:, n * 512:(n + 1) * 512], in_=pb,
                                                                func=AF.Copy, scale=scale),
                      writes=[(dst_key[id(dst)], n), pk])

    dst_key = {id(qT): "qT", id(kT): "kT", id(QB): "qB"}

    def transposes_to_oT(chunk, n, srcs, skeys):
        pb, pk = poolC.next()
        pbb = pb.bitcast(BF16)
        for j in range(4):
            P.add("pe", lambda e, j=j, pbb=pbb: e.transpose(out=pbb[:, j * 128:(j + 1) * 128], in_=srcs[j],
                                                            identity=identb[:]),
                  reads=[skeys[j], "identb"], writes=[pk])
        P.add("act", lambda e, pbb=pbb: e.activation(out=oT[:, chunk, n * 512:(n + 1) * 512], in_=pbb[:, 0:512],
                                                     func=AF.Copy),
              writes=[("oT", chunk, n), pk])

    poolS = Rot([bk(3), bk(4), bk(5), bk(6), bk(7)])

    def pipeline(units, stage_a, stage_b, look, with_index=False):
        nU = len(units)
        for i in range(nU + look):
            if i < nU:
                stage_a(units[i], i) if with_index else stage_a(units[i])
            if i >= look:
                stage_b(units[i - look], i - look) if with_index else stage_b(units[i - look])

    wctr = [0]

    def load_win(cols, force=None):
        s = wctr[0] % 2 if force is None else force
        wctr[0] += 1
        for i, c0 in enumerate(cols):
            wload(wsl[s][i], win_d[:, c0:c0 + 128], ("wsl", s, i), "wsl%d" % s)
        return s

    def merge(bi, wbr, first, after_first=None):
        wbrs = [rv(SB + 18560 + i * 1024, 1024, BF16, "p (k n) -> p k n", k=4) for i in range(2)]
        wgts = [rv(SB + 20608 + i * 2048, 2048, BF16, "p (k n) -> p k n", k=KD) for i in range(2)]
        sigs = [spr[i] for i in range(2)]
        tts = [rv(SB + 24704 + i * 2048, 2048, F32) for i in range(2)]
        gidx = {0: 2, 1: 0, 2: 1}[bi]
        ev = 0
        for c in range(KD):
            sl = c % 2
            wload(wbrs[sl], wbr[:, c * 128:(c + 1) * 128], ("wbr", sl), "wm%d" % sl)
            wload(wgts[sl], wgate_d[:, gidx * 1024 + c * 128:gidx * 1024 + (c + 1) * 128], ("wgt", sl), "wm%d" % sl)
            if c == 0 and after_first is not None:
                after_first()
            for n in range(4):
                py, pyk = poolA.next()
                pg, pgk = poolB.next()
                for k in range(4):
                    P.add("pe", lambda e, k=k, py=py, sl=sl, n=n: e.matmul(
                        py, lhsT=wbrs[sl][:, k, :], rhs=oT[:, k, n * 512:(n + 1) * 512],
                        start=(k == 0), stop=(k == 3)),
                        reads=[("wbr", sl), ("oT", k, n)], writes=[pyk])
                for k in range(KD):
                    P.add("pe", lambda e, k=k, pg=pg, sl=sl, n=n: e.matmul(
                        pg, lhsT=wgts[sl][:, k, :], rhs=hT[:, k, n * 512:(n + 1) * 512],
                        start=(k == 0), stop=(k == KD - 1)),
                        reads=[("wgt", sl)] + [("hT", t) for t in range(4 * n, 4 * n + 4)], writes=[pgk])
                sg = sigs[ev % 2]
                tt = tts[ev % 2]
                sgk, ttk = ("spr", ev % 2), ("tt", ev % 2)
                tal = [("E", 2 * (ev % 2)), ("E", 2 * (ev % 2) + 1)]
                ev += 1
                bcol = 32 + gidx * 8 + c
                P.add("act", lambda e, sg=sg, pg=pg, bcol=bcol: e.activation(
                    out=sg, in_=pg, func=AF.Sigmoid, bias=par[:, bcol:bcol + 1]),
                    reads=["par"], writes=[sgk, pgk])
                mslice = mergedT[:, c, n * 512:(n + 1) * 512]
                if first:
                    P.add("dve", lambda e, sg=sg, py=py, mslice=mslice: e.tensor_tensor(
                        out=mslice, in0=py, in1=sg, op=ALU.mult),
                        reads=[sgk], writes=[("mT", c, n), pyk])
                else:
                    P.add("dve", lambda e, sg=sg, py=py, tt=tt: e.tensor_tensor(
                        out=tt, in0=py, in1=sg, op=ALU.mult), reads=[sgk], writes=[ttk, pyk] + tal)
                    P.add("dve", lambda e, tt=tt, mslice=mslice: e.tensor_tensor(
                        out=mslice, in0=tt, in1=mslice, op=ALU.add), reads=[ttk] + tal, writes=[("mT", c, n)])

    for h in range(3):
        wload(wsl[0][h % 3], win_d[:, O_MQ + h * 128:O_MQ + (h + 1) * 128], ("wsl", 0, h % 3), "wsl0")
    wctr[0] += 4
    qbufs = [qT, QB]
    macc = Rot([bk(0), bk(1), bk(2), bk(3)])
    mscore = Rot([bk(4), bk(5), bk(6), bk(7)])
    munits = []
    for h in range(4):
        for n in range(4):
            accb = [macc.next(), macc.next()]
            u = len(munits)
            ets = [(Et[(2 * u + kt) % 4], ("E", (2 * u + kt) % 4)) for kt in range(2)]
            munits.append((h, n, accb, ets))
    proj_fm(qbufs[0], 0, wsl[0][0], ("wsl", 0, 0), 128.0 ** -0.5)
    wload(wsl[0][0], win_d[:, O_MQ + 3 * 128:O_MQ + 4 * 128], ("wsl", 0, 0), "wsl0")

    def m_stage_a(u):
        h, n, accb, ets = u
        qb_ = qbufs[h % 2]
        qk = "qT" if h % 2 == 0 else "qB"
        if n == 1 and h + 1 < 4:
            proj_fm(qbufs[(h + 1) % 2], 0, wsl[0][(h + 1) % 3], ("wsl", 0, (h + 1) % 3), 128.0 ** -0.5)
        for kt in range(2):
            pb, pk = mscore.next()
            E, ek = ets[kt]
            P.add("pe", lambda e, kt=kt, n=n, pb=pb, h=h, qb_=qb_: e.matmul(
                pb, lhsT=mkT[:, h, kt * 128:(kt + 1) * 128], rhs=qb_[:, n * 512:(n + 1) * 512],
                start=True, stop=True), reads=["mkT", (qk, n)], writes=[pk])
            P.add("act", lambda e, E=E, pb=pb: e.activation(out=E, in_=pb, func=AF.Exp),
                  writes=[ek, pk])

    def m_stage_b(u):
        h, n, accb, ets = u
        for j in range(4):
            ab, abk = accb[j // 2]
            a = ab[:, (j % 2) * 132:(j % 2) * 132 + 129]
            for kt in range(2):
                E, ek = ets[kt]
                P.add("pe", lambda e, a=a, E=E, j=j, kt=kt, h=h: e.matmul(
                    a, lhsT=E[:, j * 128:(j + 1) * 128], rhs=mv[:, kt, h, 0:129],
                    start=(kt == 0 and j % 2 == 0), stop=(kt == 1), skip_group_check=True),
                    reads=[ek, "mv2"], writes=[abk])
        for j in range(4):
            ab, abk = accb[j // 2]
            a = ab[:, (j % 2) * 132:(j % 2) * 132 + 129]
            rc = stat[:, 44 + j:45 + j]
            P.add("dve", lambda e, a=a, rc=rc: e.reciprocal(out=rc, in_=a[:, 128:129]),
                  writes=[("rc", j), abk])
            P.add("dve", lambda e, a=a, rc=rc, j=j: e.tensor_scalar(
                out=otokb[j], in0=a[:, 0:128], scalar1=rc, scalar2=None, op0=ALU.mult),
                reads=[("rc", j)], writes=[("otokb", j), abk])
        transposes_to_oT(h, n, otokb, [("otokb", j) for j in range(4)])

    pipeline(munits, m_stage_a, m_stage_b, 1)
    na_pre = {}

    def pre_na():
        na_pre[0] = (load_win([O_NQ, O_NK, O_NV], force=0), True)
        P.dma("pool", lambda e: e.dma_start(out=nabt_flat, in_=nab_d[0]), writes=["nabt"], stream="nab")
    merge(0, wbr_d[0], True, after_first=pre_na)
    vaugn = vaug[:, :, 0:130].rearrange("p t (h c) -> p t h c", h=2)
    P.add("dve", lambda e: e.memset(qT[64:128, :], 0.0), writes=[("qT", n) for n in range(4)])
    P.add("dve", lambda e: e.memset(QB[0:64, :], 0.0), writes=[("qT", n) for n in range(4)] + [("qB", n) for n in range(4)])
    for hp in range(4):
        if hp in na_pre:
            s = na_pre[hp][0]
        else:
            s = load_win([O_NQ + hp * 128, O_NK + hp * 128, O_NV + hp * 128], force=0)
            P.dma("pool", lambda e, hp=hp: e.dma_start(out=nabt_flat, in_=nab_d[hp]),
                  writes=["nabt"], stream="nab")
        proj_fm(qT, 0, wsl[s][0], ("wsl", s, 0), 0.125, split=True)
        proj_fm(kT, 0, wsl[s][1], ("wsl", s, 1), 1.0)
        P.add("dve", lambda e: e.memset(vaugn[:, :, :, 64:65], 1.0), writes=["vones"])
        for g in range(4):
            pb, pk = poolC.next()
            for tt_ in range(4):
                t = 4 * g + tt_
                for k in range(KD):
                    P.add("pe", lambda e, k=k, t=t, tt_=tt_, pb=pb, s=s: e.matmul(
                        pb[:, tt_ * 128:(tt_ + 1) * 128], lhsT=hT[:, k, t * 128:(t + 1) * 128],
                        rhs=wsl[s][2][:, k, :], start=(k == 0), stop=(k == KD - 1)),
                        reads=[("wsl", s, 2), ("hT", t)], writes=[pk])
            P.add("dve", lambda e, g=g, pb=pb: e.tensor_copy(
                out=vaugn[:, 4 * g:4 * g + 4, :, 0:64],
                in_=pb.rearrange("p (t h c) -> p t h c", t=4, h=2)),
                reads=["vones"], writes=[("v", g), pk])
        units = []
        ecnt = 0
        for t in range(NT):
            ab, abk = poolA.next()
            for hh in range(2):
                kts = na_key_tiles(t)
                chunks = [kts[0:4], kts[4:]] if len(kts) > 4 else [kts]
                es = []
                for ch in chunks:
                    es.append((Et[ecnt % 4], ("E", ecnt % 4)))
                    ecnt += 1
                units.append((t, hh, ab, abk, chunks, es))

        def stage_a(u):
            t, hh, ab, abk, chunks, es = u
            for ci, ch in enumerate(chunks):
                pb, pk = poolS.next()
                for i, (kt, var) in enumerate(ch):
                    P.add("pe", lambda e, i=i, kt=kt, hh=hh, t=t, pb=pb: e.matmul(
                        pb[:, i * 128:(i + 1) * 128], lhsT=kT[:, kt * 128:(kt + 1) * 128],
                        rhs=QAB[hh][:, t * 128:(t + 1) * 128], start=True, stop=False),
                        reads=[("kT", kt // 4), ("qT", t // 4)], writes=[pk])
                    P.add("pe", lambda e, i=i, var=var, hh=hh, pb=pb: e.matmul(
                        pb[:, i * 128:(i + 1) * 128], lhsT=identb[:], rhs=nabt[:, hh, var, :],
                        start=False, stop=True), reads=["identb", "nabt"], writes=[pk])
                E, ek = es[ci]
                w = 128 * len(ch)
                P.add("act", lambda e, E=E, pb=pb, w=w: e.activation(out=E[:, 0:w], in_=pb[:, 0:w], func=AF.Exp),
                      writes=[ek, pk])

        def stage_b(u, hp=hp):
            t, hh, ab, abk, chunks, es = u
            first_pv = True
            for ci, ch in enumerate(chunks):
                E, ek = es[ci]
                for i, (kt, var) in enumerate(ch):
                    last = (ci == len(chunks) - 1 and i == len(ch) - 1)
                    P.add("pe", lambda e, i=i, kt=kt, hh=hh, E=E, ab=ab, fp=first_pv, last=last: e.matmul(
                        ab[:, hh * 65:(hh + 1) * 65], lhsT=E[:, i * 128:(i + 1) * 128], rhs=vaugn[:, kt, hh, :],
                        start=(fp and hh == 0), stop=last, skip_group_check=True),
                        reads=[ek, ("v", kt // 4), "vones"], writes=[abk])
                    first_pv = False
            if hh == 1:
                rc = stat[:, 44:46]
                P.add("dve", lambda e, ab=ab: e.reciprocal(
                    out=rc, in_=ab[:, 0:130].rearrange("p (h c) -> p h c", h=2)[:, :, 64]),
                    writes=["rc2", abk])
                P.add("dve", lambda e, ab=ab, t=t: e.tensor_tensor(
                    out=otokb[t % 4].rearrange("p (h c) -> p h c", h=2),
                    in0=ab[:, 0:130].rearrange("p (h c) -> p h c", h=2)[:, :, 0:64],
                    in1=rc.unsqueeze(2).to_broadcast([128, 2, 64]), op=ALU.mult),
                    reads=["rc2"], writes=[("otokb", t % 4), abk])
                if t % 4 == 3:
                    transposes_to_oT(hp, t // 4, otokb, [("otokb", j) for j in range(4)])

        pipeline(units, stage_a, stage_b, 1)
    df_pre = {}

    def pre_df():
        df_pre[0] = load_win([O_DQ, O_DK, O_DV], force=0)
        P.dma("sp", lambda e: e.dma_start(out=alib, in_=alibi_d[0]), writes=["alib", "nabt"], stream="alibi")
    merge(1, wbr_d[1], False, after_first=pre_df)
    vt = vaug[:, :, 0:128]
    tA = rv(SB + 28800, 2048, F32)
    tB = rv(SB + 50816, 2048, F32)
    onesf = rv(SB + 52864, 512, F32)
    onesb = rv(SB + 53376, 256, BF16)
    P.add("dve", lambda e: e.memset(qT[64:128, :], 0.0), writes=[("qT", n) for n in range(4)])
    P.add("dve", lambda e: e.memset(QB[0:64, :], 0.0), writes=[("qT", n) for n in range(4)])
    P.add("dve", lambda e: e.memset(onesf, 1.0), writes=["onesf"])
    P.add("dve", lambda e: e.memset(onesb, 1.0), writes=["onesb"])
    sublnc = stat[:, 37:38]
    P.add("dve", lambda e: e.tensor_scalar(out=sublnc, in0=par[:, 440:441], scalar1=1.0 - LAM_INIT, scalar2=None,
                                           op0=ALU.mult), reads=["par"], writes=["sublnc"])
    poolD = Rot([bk(4), bk(5), bk(6)])
    spr3 = [spr[0], spr[1], rv(53248, 2048, F32)]
    for h in range(4):
        if h in df_pre:
            s = df_pre[h]
        else:
            s = load_win([O_DQ + h * 128, O_DK + h * 128, O_DV + h * 128], force=0)
            P.dma("sp", lambda e, h=h: e.dma_start(out=alib, in_=alibi_d[h]), writes=["alib"], stream="alibi")
        proj_fm(qT, 0, wsl[s][0], ("wsl", s, 0), 0.125, split=True)
        proj_fm(kT, 0, wsl[s][1], ("wsl", s, 1), 1.0)
        for g in range(4):
            pb, pk = poolC.next()
            for tt_ in range(4):
                t = 4 * g + tt_
                for k in range(KD):
                    P.add("pe", lambda e, k=k, t=t, tt_=tt_, pb=pb, s=s: e.matmul(
                        pb[:, tt_ * 128:(tt_ + 1) * 128], lhsT=hT[:, k, t * 128:(t + 1) * 128],
                        rhs=wsl[s][2][:, k, :], start=(k == 0), stop=(k == KD - 1)),
                        reads=[("wsl", s, 2), ("hT", t)], writes=[pk])
            P.add("dve", lambda e, g=g, pb=pb: e.tensor_copy(
                out=vt[:, 4 * g:4 * g + 4, :], in_=pb.rearrange("p (t c) -> p t c", t=4)),
                writes=[("v", g), pk])
        units = [(n, m, kt) for n in range(4) for m in range(2) for kt in range(NT)]
        OS = {0: (bk(0), bk(1)), 1: (bk(2), bk(3))}

        def stage_a(u, ui):
            n, m, kt = u
            pb, pk = poolD.next()
            P.add("pe", lambda e, m=m, kt=kt, n=n, pb=pb: e.matmul(
                pb, lhsT=kT[:, kt * 128:(kt + 1) * 128],
                rhs=QAB[m][:, n * 512:(n + 1) * 512], start=True, stop=True),
                reads=[("kT", kt // 4), ("qT", n)], writes=[pk])
            sp_, spk = spr3[ui % 3], ("spr", ui % 3)
            E, ek = Et[ui % 4], ("E", ui % 4)
            off = n * 512 - kt * 128 + 1920
            P.add("dve", lambda e, sp_=sp_, pb=pb, off=off: e.tensor_tensor(
                out=sp_, in0=pb, in1=alib[:, off:off + 512], op=ALU.add),
                reads=["alib"], writes=[spk, pk])
            P.add("act", lambda e, sp_=sp_, E=E: e.activation(out=E, in_=sp_, func=AF.Exp),
                  reads=[spk], writes=[ek])

        def stage_b(u, ui, h=h):
            n, m, kt = u
            E, ek = Et[ui % 4], ("E", ui % 4)
            (ob, obk), (sb, sbk) = OS[m]
            P.add("pe", lambda e, E=E, kt=kt, ob=ob: e.matmul(
                ob, lhsT=vt[:, kt, :], rhs=E, start=(kt == 0), stop=(kt == NT - 1)),
                reads=[ek, ("v", kt // 4)], writes=[obk])
            P.add("pe", lambda e, E=E, kt=kt, sb=sb: e.matmul(
                sb, lhsT=onesb, rhs=E, start=(kt == 0), stop=(kt == NT - 1)),
                reads=[ek, "onesb"], writes=[sbk])
            if m == 1 and kt == NT - 1:
                (o1, o1k), (s1, s1k) = OS[0]
                (o2, o2k), (s2, s2k) = OS[1]
                P.add("act", lambda e: e.activation(out=tA, in_=s1, func=AF.Ln), writes=["tA", s1k])
                P.add("act", lambda e: e.activation(out=tB, in_=s2, func=AF.Ln), writes=["tB", s2k])
                P.add("act", lambda e: e.activation(out=tA, in_=tA, func=AF.Exp, scale=-1.0),
                      reads=["tA"], writes=["tA"])
                P.add("act", lambda e: e.activation(out=tB, in_=tB, func=AF.Exp, scale=-1.0),
                      reads=["tB"], writes=["tB"])
                P.add("dve", lambda e: e.tensor_tensor(out=tA, in0=o1, in1=tA, op=ALU.mult),
                      reads=["tA"], writes=["tA", o1k])
                P.add("dve", lambda e: e.tensor_tensor(out=tB, in0=o2, in1=tB, op=ALU.mult),
                      reads=["tB"], writes=["tB", o2k])
                P.add("dve", lambda e: e.scalar_tensor_tensor(out=tA, in0=tB, scalar=nlam, in1=tA,
                                                              op0=ALU.mult, op1=ALU.add),
                      reads=["tB", "nlam", "tA"], writes=["tA"])
                P.add("act", lambda e: e.activation(out=tB, in_=tA, func=AF.Square), reads=["tA"], writes=["tB"])
                qb, qk = bk(7)
                P.add("pe", lambda e, qb=qb: e.matmul(qb, lhsT=onesf, rhs=tB, start=True, stop=True),
                      reads=["onesf", "tB"], writes=[qk])
                P.add("act", lambda e, qb=qb: e.activation(out=tB, in_=qb, func=AF.Ln, scale=1.0 / 128,
                                                            bias=epsb[:, 0:1]),
                      reads=["epsb"], writes=["tB", qk])
                P.add("act", lambda e: e.activation(out=tB, in_=tB, func=AF.Exp, scale=-0.5),
                      reads=["tB"], writes=["tB"])
                P.add("dve", lambda e, n=n, h=h: e.scalar_tensor_tensor(
                    out=oT[:, h, n * 512:(n + 1) * 512], in0=tA, scalar=sublnc, in1=tB,
                    op0=ALU.mult, op1=ALU.mult),
                    reads=["tA", "tB", "sublnc"], writes=[("oT", h, n)])

        pipeline(units, stage_a, stage_b, 3, with_index=True)
    wout = rv(SB, 16384, BF16, "p (k n) -> p k n", k=KD)
    wout_alias = ([("qT", n) for n in range(4)] + [("kT", n) for n in range(4)] + [("v", g) for g in range(4)]
                  + ["vones", ("wsl", 0, 0), ("wsl", 0, 1)])
    def pre_wout():
        P.dma("pool", lambda e: e.dma_start(out=wout, in_=wout_d[:, :].rearrange("(k p) n -> p k n", p=128)),
              writes=["wout"] + wout_alias, stream="wout")
    merge(2, wbr_d[2], False, after_first=pre_wout)

    oT_keys = [("oT", k, n) for k in range(4) for n in range(4)]
    wgs2 = [rv(24576 + i * 8192, 4096, BF16, "p (k n) -> p k n", k=KD) for i in range(3)]
    wus2 = [rv(24576 + i * 8192 + 4096, 4096, BF16, "p (k n) -> p k n", k=KD) for i in range(3)]
    for pi, sl in ((0, 1), (1, 2)):
        P.dma("pool", lambda e, pi=pi, sl=sl: e.dma_start(
            out=wgs2[sl], in_=w2g[:, pi * 256:(pi + 1) * 256].rearrange("(k p) n -> p k n", p=128)),
            writes=[("wg", sl)] + oT_keys, stream="wgu%d" % sl)
        P.dma("pool", lambda e, pi=pi, sl=sl: e.dma_start(
            out=wus2[sl], in_=w2u[:, pi * 256:(pi + 1) * 256].rearrange("(k p) n -> p k n", p=128)),
            writes=[("wu", sl)] + oT_keys, stream="wgu%d" % sl)
    pop2 = Rot([bk(4), bk(5), bk(6), bk(7)])
    for t in range(NT):
        for j in range(2):
            po, pok = pop2.next()
            for k in range(KD):
                P.add("pe", lambda e, k=k, po=po, t=t, j=j: e.matmul(
                    po, lhsT=mergedT[:, k, t * 128:(t + 1) * 128], rhs=wout[:, k, j * 512:(j + 1) * 512],
                    start=(k == 0), stop=(k == KD - 1)),
                    reads=["wout", ("mT", k, t // 4)], writes=[pok])
            P.add("dve", lambda e, po=po, t=t, j=j: e.tensor_tensor(
                out=xs[:, t, j * 512:(j + 1) * 512], in0=po, in1=xs[:, t, j * 512:(j + 1) * 512], op=ALU.add),
                writes=[("xs", t), pok])

    norm_to_hT(16, hb_f, junk_f, Rot([bkb(6), bkb(7)]))
    ffn(w2g, w2u, w2d, "f2", pidx0=1, preloaded=2)

    gfin = rv(83968, 4096, F32)
    outb = [rv(88064 + i * 4096, 4096, F32) for i in range(4)]
    P.dma("sp", lambda e: e.dma_start(out=gfin, in_=gfin_d[:, :]), reads=[("xs", 0)], writes=["gfin"], stream="gfin")
    junk_o = rv(104448, 2048, BF16)
    for g in range(4):
        for t in range(4 * g, 4 * g + 4):
            P.add("act", lambda e, t=t: e.activation(out=junk_o, in_=xs[:, t, :], func=AF.Square,
                                                      accum_out=ss[:, t:t + 1]),
                  reads=[("xs", t)], writes=["junk", ("ss", t)])
        rstd_ops(ss[:, 4 * g:4 * g + 4], rstd[:, 4 * g:4 * g + 4], D, [("ss", u) for u in range(4 * g, 4 * g + 4)],
                 ("rstd", g), 4)
        for t in range(4 * g, 4 * g + 4):
            ob = outb[t % 4]
            P.add("dve", lambda e, t=t, ob=ob: e.scalar_tensor_tensor(
                out=ob, in0=xs[:, t, :], scalar=rstd[:, t:t + 1], in1=gfin, op0=ALU.mult, op1=ALU.mult),
                reads=[("xs", t), ("rstd", g), "gfin"], writes=[("ob", t % 4)])
            P.dma("sp", lambda e, t=t, ob=ob: e.dma_start(out=y_d[t * 128:(t + 1) * 128, :], in_=ob),
                  reads=[("ob", t % 4)], stream="out%d" % (t % 4))
    counts = P.emit(final_wait_streams=["out0", "out1", "out2", "out3"])
    return nc, counts


_CACHE = {}


def kernel(x, mem, ffn1_norm, ffn1_w_gate, ffn1_w_up, ffn1_w_down, mix_norm, w_in, na_rpb,
           diff_lambda_q1, diff_lambda_k1, diff_lambda_q2, diff_lambda_k2, diff_subln,
           mem_norm, w_mem_kv, w_gate, b_gate, w_br_na, w_br_diff, w_br_mem, w_out,
           ffn2_norm, ffn2_w_gate, ffn2_w_up, ffn2_w_down, final_norm):
    f = lambda a: np.ascontiguousarray(np.asarray(a, dtype=np.float32))
    x = f(x)
    mem = f(mem)
    colT = lambda v: f(v).reshape(-1, 128).T
    rep = lambda v: np.broadcast_to(f(v).reshape(1, -1), (128, f(v).size))
    par = np.concatenate([
        colT(ffn1_norm[0]), colT(mix_norm[0]), colT(ffn2_norm[0]), colT(mem_norm[0]), colT(b_gate[0]),
        rep(diff_lambda_q1[0]), rep(diff_lambda_k1[0]), rep(diff_lambda_q2[0]), rep(diff_lambda_k2[0]),
        rep(diff_subln[0]), colT(diff_subln[0])], axis=1)
    par = np.ascontiguousarray(par, dtype=np.float32)
    assert par.shape == (128, 441)
    gfin = np.ascontiguousarray(rep(final_norm))
    shared = {
        "ffn1_w_gate": f(ffn1_w_gate)[0], "ffn1_w_up": f(ffn1_w_up)[0], "ffn1_w_down": f(ffn1_w_down)[0],
        "ffn2_w_gate": f(ffn2_w_gate)[0], "ffn2_w_up": f(ffn2_w_up)[0], "ffn2_w_down": f(ffn2_w_down)[0],
        "w_in": f(w_in)[0], "w_mem_kv": f(w_mem_kv)[0], "w_gate": f(w_gate)[0],
        "w_br_mem": f(w_br_mem)[0], "w_br_na": f(w_br_na)[0], "w_br_diff": f(w_br_diff)[0],
        "w_out": f(w_out)[0], "params": par, "gfin": gfin,
        "ident": np.eye(128, dtype=np.float32), "alibi": build_alibi(),
        "nab": np.ascontiguousarray(build_nab(f(na_rpb)[0]).reshape(4, 2, 21, 128, 128).transpose(0, 3, 1, 2, 4)
                                    .reshape(4, 128, 2 * 21 * 128)),
    }
    if "nc" not in _CACHE:
        _CACHE["nc"] = build_nc()[0]
    nc = _CACHE["nc"]
    in_maps = []
    for b in range(8):
        m = dict(shared)
        m["x"] = x[b]
        m["mem"] = mem[b]
        in_maps.append(m)
    res = run_bass_kernel_spmd(nc, in_maps, core_ids=list(range(8)))
    return np.stack([np.asarray(r["y"], dtype=np.float32) for r in res.results], axis=0)
```
